# Optimizing a Trainium2 kernel written in Bass

```python
import math
import jax, jax.numpy as jnp
from jax import lax
import numpy as np

D_MODEL = 1024
BATCH = 16
SEQ = 4096
DEPTH = 4

GRID_W = 64
CTX_LEN = 256
POOL_WIDTH = 512
POOL_GROUPS = 4
POOL_WINDOWS = (2, 4, 8, 16)
GLA_HEADS = 4
GLA_DK = 128
GLA_DV = 256
QK_W = GLA_HEADS * GLA_DK
V_W = GLA_HEADS * GLA_DV
GLA_GATE_RANK = 16
GLA_GATE_TEMP = 16.0
GLA_CHUNK = 64
FFN_HIDDEN = 2816
CONV_WIDTH = 3
DEEPNORM_ALPHA = (2.0 * DEPTH) ** 0.25
DEEPNORM_BETA = (8.0 * DEPTH) ** -0.25
LN_EPS = 1e-6
IN_WIDTHS = (POOL_WIDTH, QK_W, QK_W, V_W, V_W, GLA_GATE_RANK, GLA_GATE_RANK, D_MODEL, D_MODEL)
IN_SPLITS = tuple(int(s) for s in np.cumsum(IN_WIDTHS)[:-1])
N_IN = int(sum(IN_WIDTHS))

kernel_name = "hybrid_pool_gla_convffn_dit"


def layer_norm(x, w=None, b=None):
    xf = x.astype(jnp.float32)
    mu = jnp.mean(xf, -1, keepdims=True)
    var = jnp.mean(jnp.square(xf - mu), -1, keepdims=True)
    y = (xf - mu) * lax.rsqrt(var + LN_EPS)
    if w is not None:
        y = y * w.astype(jnp.float32) + b.astype(jnp.float32)
    return y.astype(x.dtype)


def pos_embed_2d(rows, cols, dim):
    quarter = dim // 4
    omega = 1.0 / (10000.0 ** (jnp.arange(quarter, dtype=jnp.float32) / quarter))
    r = jnp.arange(rows, dtype=jnp.float32)[:, None] * omega
    cl = jnp.arange(cols, dtype=jnp.float32)[:, None] * omega
    er = jnp.concatenate([jnp.sin(r), jnp.cos(r)], -1)
    ec = jnp.concatenate([jnp.sin(cl), jnp.cos(cl)], -1)
    emb = jnp.concatenate([jnp.broadcast_to(er[:, None, :], (rows, cols, dim // 2)),
                           jnp.broadcast_to(ec[None, :, :], (rows, cols, dim // 2))], -1)
    return emb.reshape(rows * cols, dim)


def pool_minus_self(p, axis, window):
    L = p.shape[axis]
    pf = p.astype(jnp.float32)
    pad_cfg = [(0, 0)] * p.ndim
    pad_cfg[axis] = (1, 0)
    cs = jnp.pad(jnp.cumsum(pf, axis=axis), pad_cfg)
    t = jnp.arange(L)
    lo = jnp.clip(t - window // 2, 0, L)
    hi = jnp.clip(t + window // 2, 0, L)
    s = jnp.take(cs, hi, axis=axis) - jnp.take(cs, lo, axis=axis)
    shape = [1] * p.ndim
    shape[axis] = L
    cnt = (hi - lo).astype(jnp.float32).reshape(shape)
    return (s / cnt - pf).astype(p.dtype)


def pool_branch(p, grid, w_pool, pool_scale):
    B, L, Cw = p.shape
    if grid:
        view, axis = p.reshape(B, L // GRID_W, GRID_W, Cw), 2
    else:
        view, axis = p, 1
    gs = Cw // POOL_GROUPS
    ys = [pool_minus_self(view[..., g * gs:(g + 1) * gs], axis, POOL_WINDOWS[g]) for g in range(POOL_GROUPS)]
    y = jnp.stack(ys, axis=-2).reshape(B, L, POOL_GROUPS, gs)
    y = jnp.einsum('blgc,gcd->blgd', y, w_pool).reshape(B, L, Cw)
    return y * pool_scale


def dwconv(h, conv_w, conv_b, grid):
    B, L, F = h.shape
    if grid:
        view, axis = h.reshape(B, L // GRID_W, GRID_W, F), 2
    else:
        view, axis = h, 1
    n = view.shape[axis]
    half = CONV_WIDTH // 2
    pad_cfg = [(0, 0)] * view.ndim
    pad_cfg[axis] = (half, half)
    hp = jnp.pad(view, pad_cfg)
    out = sum(conv_w[k] * lax.slice_in_dim(hp, k, k + n, axis=axis) for k in range(CONV_WIDTH))
    return (out + conv_b).reshape(B, L, F)


def conv_ffn(u, grid, w_up, conv_w, conv_b, w_down):
    a, gt = jnp.split(u @ w_up, 2, axis=-1)
    a = dwconv(a, conv_w, conv_b, grid)
    return (jax.nn.gelu(a, approximate=False) * gt) @ w_down


def gla_chunk(q, k, v, log_a, s0):
    B, L, H, DK = q.shape
    DV = v.shape[-1]
    C = GLA_CHUNK
    N = L // C
    f32 = jnp.float32
    q = q.astype(f32).reshape(B, N, C, H, DK)
    k = k.astype(f32).reshape(B, N, C, H, DK)
    v = v.astype(f32).reshape(B, N, C, H, DV)
    bcum = jnp.cumsum(log_a.astype(f32).reshape(B, N, C, H, DK), axis=2)
    b_last = bcum[:, :, -1:]
    qe = q * jnp.exp(bcum)
    ke = k * jnp.exp(-bcum)
    kend = k * jnp.exp(b_last - bcum)
    mask = jnp.tril(jnp.ones((C, C), dtype=bool))
    att = jnp.where(mask, jnp.einsum('bnihd,bnjhd->bnhij', qe, ke), 0.0)
    o_intra = jnp.einsum('bnhij,bnjhv->bnihv', att, v)

    def step(S, xs):
        qe_n, kend_n, v_n, dec_n = xs
        o_n = jnp.einsum('bihd,bhdv->bihv', qe_n, S)
        S = dec_n[..., None] * S + jnp.einsum('bjhd,bjhv->bhdv', kend_n, v_n)
        return S, o_n

    xs = (jnp.moveaxis(qe, 1, 0), jnp.moveaxis(kend, 1, 0), jnp.moveaxis(v, 1, 0),
          jnp.moveaxis(jnp.exp(b_last[:, :, 0]), 1, 0))
    s_fin, o_inter = lax.scan(step, s0.astype(f32), xs)
    o = o_intra + jnp.moveaxis(o_inter, 0, 1)
    return o.reshape(B, L, H, DV), s_fin


def flip_seq(a):
    return jnp.flip(a, axis=1)


def bi_gla(q, k, v, la_f, la_b, s_f0, s_b0):
    o_f, s_f = gla_chunk(q, k, v, la_f, s_f0)
    o_b, s_b = gla_chunk(flip_seq(q), flip_seq(k), flip_seq(v), flip_seq(la_b), s_b0)
    return o_f + flip_seq(o_b), s_f, s_b


def gla_inputs(q, k, v, zf, zb, w_gate_f, b_gate_f, w_gate_b, b_gate_b):
    B, L = q.shape[:2]
    q = q.reshape(B, L, GLA_HEADS, GLA_DK) * (GLA_DK ** -0.5)
    k = k.reshape(B, L, GLA_HEADS, GLA_DK)
    v = v.reshape(B, L, GLA_HEADS, GLA_DV)
    la_f = (jax.nn.log_sigmoid((zf @ w_gate_f + b_gate_f).astype(jnp.float32)) / GLA_GATE_TEMP).reshape(B, L, GLA_HEADS, GLA_DK)
    la_b = (jax.nn.log_sigmoid((zb @ w_gate_b + b_gate_b).astype(jnp.float32)) / GLA_GATE_TEMP).reshape(B, L, GLA_HEADS, GLA_DK)
    return q, k, v, la_f, la_b


def mixer_out(pool, r, g_pool, g_gla, o, grid, w_pool, pool_scale, gla_norm_w, w_br_pool, w_br_gla, w_out):
    B, L = pool.shape[:2]
    y_pool = pool_branch(pool, grid, w_pool, pool_scale) @ w_br_pool
    of = o * lax.rsqrt(jnp.mean(jnp.square(o), -1, keepdims=True) + LN_EPS)
    of = of.reshape(B, L, V_W) * gla_norm_w.astype(jnp.float32)
    y_gla = (of.astype(r.dtype) * jax.nn.silu(r)) @ w_br_gla
    merged = jax.nn.sigmoid(g_pool) * y_pool + jax.nn.sigmoid(g_gla) * y_gla
    return merged @ w_out


def setup_inputs(seed: int = 0) -> dict:
    key = jax.random.key(seed)
    ks = jax.random.split(key, 25)
    f32 = jnp.float32

    def nrm(k, shape, scale):
        return jax.random.normal(k, shape, f32) * scale

    D, F = D_MODEL, FFN_HIDDEN
    return {
        "x": nrm(ks[0], (BATCH, SEQ, D), 1.0),
        "c": nrm(ks[1], (BATCH, D), 1.0),
        "ctx": nrm(ks[2], (BATCH, CTX_LEN, D), 1.0),
        "c_ctx": nrm(ks[3], (D,), 1.0),
        "w_mod": nrm(ks[4], (DEPTH, D, 6 * D), 0.5 * D ** -0.5),
        "b_mod": nrm(ks[5], (DEPTH, 6 * D), 0.02),
        "w_in": nrm(ks[6], (DEPTH, D, N_IN), D ** -0.5),
        "w_gate_f": nrm(ks[7], (DEPTH, GLA_GATE_RANK, QK_W), GLA_GATE_RANK ** -0.5),
        "b_gate_f": 2.0 + nrm(ks[8], (DEPTH, QK_W), 0.1),
        "w_gate_b": nrm(ks[9], (DEPTH, GLA_GATE_RANK, QK_W), GLA_GATE_RANK ** -0.5),
        "b_gate_b": 2.0 + nrm(ks[10], (DEPTH, QK_W), 0.1),
        "gla_norm_w": 1.0 + nrm(ks[11], (DEPTH, V_W), 0.02),
        "w_pool": nrm(ks[12], (DEPTH, POOL_GROUPS, POOL_WIDTH // POOL_GROUPS, POOL_WIDTH // POOL_GROUPS), (POOL_WIDTH // POOL_GROUPS) ** -0.5),
        "pool_scale": 1.0 + nrm(ks[13], (DEPTH, POOL_WIDTH), 0.02),
        "w_br_pool": nrm(ks[14], (DEPTH, POOL_WIDTH, D), POOL_WIDTH ** -0.5),
        "w_br_gla": nrm(ks[15], (DEPTH, V_W, D), V_W ** -0.5),
        "w_out": nrm(ks[16], (DEPTH, D, D), DEEPNORM_BETA * D ** -0.5),
        "ln1_w": 1.0 + nrm(ks[17], (DEPTH, D), 0.02),
        "ln1_b": nrm(ks[18], (DEPTH, D), 0.02),
        "w_up": nrm(ks[19], (DEPTH, D, 2 * F), D ** -0.5),
        "conv_w": nrm(ks[20], (DEPTH, CONV_WIDTH, F), CONV_WIDTH ** -0.5),
        "conv_b": nrm(ks[21], (DEPTH, F), 0.02),
        "w_down": nrm(ks[22], (DEPTH, F, D), DEEPNORM_BETA * F ** -0.5),
        "ln2_w": 1.0 + nrm(ks[23], (DEPTH, D), 0.02),
        "ln2_b": nrm(ks[24], (DEPTH, D), 0.02),
    }


def reference(x, c, ctx, c_ctx, w_mod, b_mod, w_in, w_gate_f, b_gate_f, w_gate_b, b_gate_b,
              gla_norm_w, w_pool, pool_scale, w_br_pool, w_br_gla, w_out, ln1_w, ln1_b,
              w_up, conv_w, conv_b, w_down, ln2_w, ln2_b):
    B, L, D = x.shape
    rows = L // GRID_W
    x = layer_norm(x + pos_embed_2d(rows, GRID_W, D).astype(x.dtype)[None])
    h = layer_norm(ctx)
    s0 = jnp.zeros((B, GLA_HEADS, GLA_DK, GLA_DV), jnp.float32)

    for l in range(DEPTH):
        last = l == DEPTH - 1
        mx = (jax.nn.silu(c) @ w_mod[l] + b_mod[l])[:, None, :]
        mc = jax.nn.silu(c_ctx) @ w_mod[l] + b_mod[l]
        sh1x, sc1x, g1x, sh2x, sc2x, g2x = jnp.split(mx, 6, axis=-1)
        sh1c, sc1c, g1c, sh2c, sc2c, g2c = jnp.split(mc, 6, axis=-1)

        ux = x * (1.0 + sc1x) + sh1x
        uc = h * (1.0 + sc1c) + sh1c
        px = jnp.split(ux @ w_in[l], IN_SPLITS, axis=-1)
        pc = jnp.split(uc @ w_in[l], IN_SPLITS, axis=-1)
        gx = gla_inputs(px[1], px[2], px[3], px[5], px[6], w_gate_f[l], b_gate_f[l], w_gate_b[l], b_gate_b[l])
        gc = gla_inputs(pc[1], pc[2], pc[3], pc[5], pc[6], w_gate_f[l], b_gate_f[l], w_gate_b[l], b_gate_b[l])
        o_c, s_f, s_b = bi_gla(*gc, s0, s0)
        o_x, _, _ = bi_gla(*gx, s_f, s_b)

        mix_x = mixer_out(px[0], px[4], px[7], px[8], o_x, True, w_pool[l], pool_scale[l],
                          gla_norm_w[l], w_br_pool[l], w_br_gla[l], w_out[l])
        x = layer_norm(DEEPNORM_ALPHA * x + g1x * mix_x, ln1_w[l], ln1_b[l])

        ux2 = x * (1.0 + sc2x) + sh2x
        x = layer_norm(DEEPNORM_ALPHA * x + g2x * conv_ffn(ux2, True, w_up[l], conv_w[l], conv_b[l], w_down[l]),
                       ln2_w[l], ln2_b[l])

        if not last:
            mix_c = mixer_out(pc[0], pc[4], pc[7], pc[8], o_c, False, w_pool[l], pool_scale[l],
                              gla_norm_w[l], w_br_pool[l], w_br_gla[l], w_out[l])
            h = layer_norm(DEEPNORM_ALPHA * h + g1c * mix_c, ln1_w[l], ln1_b[l])
            uc2 = h * (1.0 + sc2c) + sh2c
            h = layer_norm(DEEPNORM_ALPHA * h + g2c * conv_ffn(uc2, False, w_up[l], conv_w[l], conv_b[l], w_down[l]),
                           ln2_w[l], ln2_b[l])
    return x
```

```python
import numpy as np
import concourse.bass as bass
import concourse.mybir as mybir
from concourse.bass_utils import run_bass_kernel_spmd

F32, BF16 = mybir.dt.float32, mybir.dt.bfloat16
AF = mybir.ActivationFunctionType
ALU = mybir.AluOpType

D = 1024
L = 4096
CTXL = 256
DEPTH = 4
NBC = 2
NCORE = 8
T = 256
NCH = T // 128
NT = L // T
KC = 8
FC = 22
HEADS = 4
ALPHA = float((2.0 * DEPTH) ** 0.25)
EPS = 1e-6
QSCALE = float(128 ** -0.5)
NSLOT = 4
SLABW = 4096
S_Z, S_Q, S_K, S_V0, S_V1, S_POOL, S_R0, S_GP0, S_GG0 = 0, 1, 2, 3, 4, 5, 6, 8, 10
S_BRPOOL, S_BRGLA0, S_OUT0, S_UP0, S_DOWN0, S_MISC, S_MOD0 = 12, 13, 15, 17, 28, 34, 35
NSLAB = 47
NVEC = 180
V_BMOD, V_LN1W, V_LN1B, V_LN2W, V_LN2B, V_GNW, V_PSC, V_CW, V_CB = 0, 48, 56, 64, 72, 80, 88, 92, 158


class Buf:
    __slots__ = ("w", "r", "name")

    def __init__(self, name=""):
        self.w = None
        self.r = {}
        self.name = name


class Owner:
    def __init__(self, name, sem, h=None):
        self.name, self.sem, self.h, self.cnt, self.seen = name, sem, h, 0, {}


class K:
    def __init__(self, nc, sems):
        self.nc = nc
        self.pe = Owner("pe", sems["pe"], nc.tensor)
        self.act = Owner("act", sems["act"], nc.scalar)
        self.dve = Owner("dve", sems["dve"], nc.vector)
        self.pool = Owner("pool", sems["pool"], nc.gpsimd)
        self.sp = Owner("sp", sems["sp"], nc.sync)
        self.sems = sems
        self.dsem = {}

    def dma_owner(self, name):
        if name not in self.dsem:
            self.dsem[name] = Owner("dma_" + name, self.sems["dma_" + name])
        return self.dsem[name]

    def wait(self, E, tok, hazard=False):
        if tok is None:
            return
        o, v = tok
        if o is E and not hazard:
            return
        if E.seen.get(o, 0) >= v:
            return
        E.h.wait_ge(o.sem, v)
        E.seen[o] = v

    def deps(self, E, ins, outs, hazard=False):
        for b in ins:
            self.wait(E, b.w, hazard)
        for b in outs:
            self.wait(E, b.w, hazard)
            for t in list(b.r.values()):
                self.wait(E, t, False)

    def record(self, tok, ins, outs):
        o = tok[0]
        for b in ins:
            b.r[o] = tok
        for b in outs:
            b.w = tok
            b.r = {}

    def op(self, E, instr_fn, ins=(), outs=(), hazard=False, sig=True):
        self.deps(E, ins, outs, hazard)
        i = instr_fn()
        if sig:
            E.cnt += 1
            i.then_inc(E.sem, 1)
            tok = (E, E.cnt)
        else:
            tok = (E, E.cnt + 1)
        self.record(tok, ins, outs)
        return tok

    def dma(self, Q, dname, out, in_, ins=(), outs=()):
        self.deps(Q, ins, outs)
        ds = self.dma_owner(dname)
        i = Q.h.dma_start(out=out, in_=in_)
        ds.cnt += 16
        i.then_inc(ds.sem, 16)
        tok = (ds, ds.cnt)
        self.record(tok, ins, outs)
        return tok


class StopBuild(Exception):
    pass


class Tl:
    def __init__(self, t, nchunk=1, name=""):
        self.t = t
        self.b = [Buf(f"{name}{i}") for i in range(nchunk)]

    @property
    def all(self):
        return self.b


DMA_SEMS = ["w0", "w1", "w2", "w3", "xa0", "xa1", "ob", "pos", "xin", "xst0", "xst1", "ost", "outst",
            "const", "cast0", "cast1", "cast2", "cast3"]


def build_program(depth=DEPTH, debug=False, stop=None, ntiles=None, force_final=False):
    nc = bass.Bass("TRN2", target_bir_lowering=False)
    dk = "ExternalOutput" if debug else "Internal"
    x_d = nc.dram_tensor("x", [NBC, L, D], F32, kind="ExternalInput").ap()
    ctx_d = nc.dram_tensor("ctx", [NBC, CTXL, D], F32, kind="ExternalInput").ap()
    cT_d = nc.dram_tensor("cT", [128, KC, 4], F32, kind="ExternalInput").ap()
    pos_d = nc.dram_tensor("posT", [NT, 128, KC, T], F32, kind="ExternalInput").ap()
    ws_d = nc.dram_tensor("wslab", [depth, NSLAB, 128, SLABW], F32, kind="ExternalInput").ap()
    vec_d = nc.dram_tensor("vec", [128, depth, NVEC], F32, kind="ExternalInput").ap()
    cm_d = nc.dram_tensor("cmat", [128, 34, 128], F32, kind="ExternalInput").ap()
    out_d = nc.dram_tensor("out", [NBC, L, D], F32, kind="ExternalOutput").ap()
    wb_d = nc.dram_tensor("wbf", [depth, NSLAB, 128, SLABW], BF16, kind="Internal").ap()
    xs_d = nc.dram_tensor("xs", [NBC, NT + 1, 128, KC, T], F32, kind=dk).ap()
    ob_d = nc.dram_tensor("obs", [NBC, NT, 128, KC, T], F32, kind=dk).ap()

    import contextlib
    es = contextlib.ExitStack()
    with es:
        def sb(name, shape, dt):
            return es.enter_context(nc.sbuf_tensor("s_" + name, shape, dt))

        sems = {}
        for n in ["pe", "act", "dve", "pool", "sp"] + ["dma_" + d for d in DMA_SEMS]:
            sems[n] = es.enter_context(nc.semaphore(n))
        k = K(nc, sems)
        PE, ACT, DVE, POOL, SP = k.pe, k.act, k.dve, k.pool, k.sp

        cm32 = Tl(sb("ident32", [128, 128], F32), 1, "ident32")
        cmb = Tl(sb("cmb", [128, 34, 128], BF16), 1, "cmb")
        vec = Tl(sb("vec", [128, depth, NVEC], F32), 1, "vec")
        lwt = Tl(sb("lwt", [128, depth, 1024], BF16), 1, "lwt")
        wgbt = Tl(sb("wgbt", [33, depth, 512], BF16), 1, "wgbt")
        cT = Tl(sb("cT", [128, KC, 4], F32), 1, "cT")
        csil = Tl(sb("csil", [128, KC, 4], BF16), 1, "csil")
        mod = Tl(sb("mod", [128, depth, 3, 48], F32), 1, "mod")
        drv = Tl(sb("drv", [128, depth, 3, 16], F32), 1, "drv")
        drv2 = Tl(sb("drv2", [128, depth, 32], F32), 1, "drv2")
        maskr = Tl(sb("maskr", [128, 2, HEADS, 128], BF16), 1, "maskr")

        wring = [Tl(sb(f"wr{i}", [128, SLABW], BF16), 1, f"wr{i}") for i in range(NSLOT)]

        xa = [Tl(sb(f"xa{i}", [128, KC, T], F32), KC, f"xa{i}") for i in range(2)]
        class Cur:
            def __init__(self, tiles):
                self.tiles, self.i = tiles, 0

            @property
            def t(self):
                return self.tiles[self.i].t

            @property
            def b(self):
                return self.tiles[self.i].b

        u = Cur([Tl(sb(f"u{i}", [128, KC, T], BF16), KC, f"u{i}") for i in range(2)])
        zT = Tl(sb("zT", [33, 2, T], BF16), 1, "zT")
        qT = Tl(sb("qT", [128, HEADS, T], BF16), 1, "qT")
        kT = Tl(sb("kT", [128, HEADS, T], BF16), 1, "kT")
        vtm = Tl(sb("vtm", [128, NCH, 1024], BF16), NCH, "vtm")
        ptm = Tl(sb("ptm", [128, NCH, 512], BF16), NCH, "ptm")
        rs = Tl(sb("rs", [128, KC, T], BF16), KC, "rs")
        gp = Tl(sb("gp", [128, KC, T], BF16), KC, "gp")
        gg = Tl(sb("gg", [128, KC, T], BF16), KC, "gg")
        o_t = sb("o", [128, KC * T], F32)
        o = Tl(o_t[:].rearrange("p (k t) -> p k t", k=KC), 1, "o")
        rstd = Tl(sb("rstd", [128, HEADS, T], F32), 1, "rstd")
        otmp = Tl(sb("otmp", [128, KC, T], F32), KC, "otmp")
        pooledT = Tl(sb("pooledT", [128, 4, T], BF16), 4, "pooledT")
        pooled2 = Tl(sb("pooled2", [128, 4, T], BF16), 4, "pooled2")
        merged = Tl(sb("merged", [128, KC, T], BF16), KC, "merged")
        ybf = Tl(sb("ybf", [128, KC, T], BF16), KC, "ybf")
        ysq = Tl(sb("ysq", [128, KC, T], BF16), KC, "ysq")
        lnm = Tl(sb("lnm", [128, T], F32), 1, "lnm")
        lnm2 = Tl(sb("lnm2", [128, T], F32), 1, "lnm2")
        lnv = Tl(sb("lnv", [128, T], F32), 1, "lnv")
        lnr = Tl(sb("lnr", [128, T], F32), 1, "lnr")
        lnt = Tl(sb("lnt", [128, KC, T], F32), KC, "lnt")
        t1, t2, of = lnt, otmp, ysq
        osq = Tl(ybf.t, 1, "osq")
        osq.b = ybf.b
        asb = [Tl(sb(f"asb{i}", [128, 2, T], F32), 1, f"asb{i}") for i in range(2)]
        acc = [Tl(sb(f"acc{i}", [128, 2, T], F32), 1, f"acc{i}") for i in range(2)]
        gel = asb
        xin = Tl(o_t[:].rearrange("p (c d) -> p c d", c=NCH), 1, "xin")
        xin.b = o.b
        xout = xin
        e1_ = Tl(sb("e1", [128, 512], F32), 1, "e1")
        e1 = [e1_, e1_]
        sp_ = [Tl(sb(f"sp{i}", [128, 512], BF16), 1, f"sp{i}") for i in range(2)]
        Ep = [Tl(sb(f"Ep{i}", [128, HEADS, 128], F32), 1, f"Ep{i}") for i in range(2)]
        Em = [Tl(sb(f"Em{i}", [128, HEADS, 128], F32), 1, f"Em{i}") for i in range(2)]
        qe = [Tl(sb(f"qe{i}", [128, HEADS, 128], BF16), 1, f"qe{i}") for i in range(2)]
        ke = [Tl(sb(f"ke{i}", [128, HEADS, 128], BF16), 1, f"ke{i}") for i in range(2)]
        er = [Tl(sb(f"er{i}", [128, 512], F32), 1, f"er{i}") for i in range(2)]
        kend = [Tl(sb(f"kend{i}", [128, 512], BF16), 1, f"kend{i}") for i in range(2)]
        attm = [Tl(sb(f"attm{i}", [128, HEADS, 128], BF16), 1, f"attm{i}") for i in range(2)]
        Sst = [Tl(sb(f"S{i}", [128, HEADS, 256], F32), 1, f"S{i}") for i in range(2)]
        Sbf = [Tl(sb(f"Sbf{i}", [128, HEADS, 256], BF16), 1, f"Sbf{i}") for i in range(2)]

        psb = [Tl(es.enter_context(nc.psum_tensor(f"ps{i}", [128, 512], F32)), 1, f"ps{i}") for i in range(7)]
        pst_t = es.enter_context(nc.psum_tensor("pst", [128, 1024], BF16))
        pst = [Buf("pst0"), Buf("pst1")]
        psn = [0]

        def getps():
            p = psb[psn[0] % 7]
            psn[0] += 1
            return p

        ckcnt = {}

        def ck(name):
            if stop is None:
                return
            nm, _, occ = stop.partition(':')
            if nm == name:
                ckcnt[name] = ckcnt.get(name, 0) + 1
                if ckcnt[name] >= int(occ or 1):
                    raise StopBuild()

        def mm(out, lhsT, rhs, start, stop, ins, outs, sig=None):
            if sig is None:
                sig = stop
            return k.op(PE, lambda: nc.tensor.matmul(out, lhsT=lhsT, rhs=rhs, start=start, stop=stop),
                        ins=ins, outs=outs, sig=sig)

        def act(out, in_, func, ins, outs, scale=1.0, bias=0.0, hazard=False):
            return k.op(ACT, lambda: nc.scalar.activation(out=out, in_=in_, func=func, bias=bias, scale=scale),
                        ins=ins, outs=outs, hazard=hazard)

        def tt(E, out, in0, in1, op, ins, outs):
            return k.op(E, lambda: E.h.tensor_tensor(out=out, in0=in0, in1=in1, op=op), ins=ins, outs=outs)

        def ts(E, out, in0, s1, s2, op0, op1, ins, outs, hazard=False):
            if s2 is None:
                return k.op(E, lambda: E.h.tensor_scalar(out=out, in0=in0, scalar1=s1, scalar2=None, op0=op0),
                            ins=ins, outs=outs, hazard=hazard)
            return k.op(E, lambda: E.h.tensor_scalar(out=out, in0=in0, scalar1=s1, scalar2=s2, op0=op0, op1=op1),
                        ins=ins, outs=outs, hazard=hazard)

        def stt(E, out, in0, scalar, in1, op0, op1, ins, outs):
            return k.op(E, lambda: E.h.scalar_tensor_tensor(out=out, in0=in0, scalar=scalar, in1=in1, op0=op0, op1=op1),
                        ins=ins, outs=outs)

        def cp(E, out, in_, ins, outs):
            if E is ACT:
                return act(out, in_, AF.Copy, ins, outs)
            return k.op(E, lambda: E.h.tensor_copy(out=out, in_=in_), ins=ins, outs=outs)

        plan = []
        wstate = {"issued": 0, "next": 0}
        wbuf_d = [[Buf(f"wbf{l}_{s}") for s in range(NSLAB)] for l in range(depth)]

        def tiles_schedule():
            for l in range(depth):
                for b in range(NBC):
                    yield (l, b, "ctx", NT)
                    for t in range(NT - 1, -1, -1):
                        yield (l, b, "bwd", t)
                    for t in range(NT):
                        yield (l, b, "fwd", t)

        for l in range(depth):
            for s in range(12):
                plan.append((l, S_MOD0 + s))
        full_sched = list(tiles_schedule())
        for i, (l, b, kind, t) in enumerate(full_sched):
            if i == 0:
                plan.extend((l, s) for s in range(3))
            nl = full_sched[i + 1][0] if i + 1 < len(full_sched) else None
            plan.extend((l, s) for s in (S_V0, S_V1))
            if kind == "bwd" or (kind == "ctx" and l == depth - 1):
                if nl is not None:
                    plan.extend((nl, s) for s in range(3))
            else:
                plan.extend((l, s) for s in range(5, 17))
                if nl is not None:
                    plan.extend((nl, s) for s in range(3))
                plan.extend((l, s) for s in range(17, 34))

        def wnext(l, s):
            n = wstate["next"]
            assert plan[n] == (l, s), (plan[n], (l, s), n)
            wstate["next"] += 1
            while wstate["issued"] < min(len(plan), n + NSLOT):
                j = wstate["issued"]
                pl, ps_ = plan[j]
                slot = wring[j % NSLOT]
                k.dma(SP, f"w{j % NSLOT}", slot.t[:], wb_d[pl, ps_], ins=[wbuf_d[pl][ps_]], outs=slot.all)
                wstate["issued"] += 1
            return wring[n % NSLOT]

        for l in range(depth):
            for s in range(NSLAB):
                k.dma(POOL, f"cast{l}", wb_d[l, s], ws_d[l, s], ins=[], outs=[wbuf_d[l][s]])
                if s % 4 == 3:
                    co = k.dma_owner(f"cast{l}")
                    k.wait(POOL, (co, co.cnt - 32))
            fin = (k.dma_owner(f"cast{l}"), k.dma_owner(f"cast{l}").cnt)
            for s in range(NSLAB):
                wbuf_d[l][s].w = fin
        k.dma(POOL, "const", cm32.t[:], cm_d[:, 0, :], outs=cm32.all)
        k.dma(POOL, "const", cmb.t[:], cm_d, outs=cmb.all)
        k.dma(POOL, "const", vec.t[:], vec_d, outs=vec.all)
        k.dma(POOL, "const", cT.t[:], cT_d, outs=cT.all)
        for l in range(depth):
            k.dma(POOL, "const", lwt.t[:, l, :], wb_d[l, S_MISC][:, 0:1024], ins=[wbuf_d[l][S_MISC]], outs=lwt.all)
            k.dma(POOL, "const", wgbt.t[:, l, :], wb_d[l, S_MISC][0:33, 2048:2560], ins=[wbuf_d[l][S_MISC]], outs=wgbt.all)
        fin = (k.dma_owner("const"), k.dma_owner("const").cnt)
        for tl in (cm32, cmb, vec, cT, lwt, wgbt):
            tl.b[0].w = fin

        for dr in range(2):
            for h in range(HEADS):
                cp(DVE, maskr.t[:, dr, h, :], cmb.t[:, 5 + dr, :], cmb.all, maskr.all)
        ident32 = cm32.t[:]
        identb = cmb.t[:, 0, :]
        Umat = [cmb.t[:, 1, :], cmb.t[:, 2, :]]
        W2mat = [cmb.t[:, 3, :], cmb.t[:, 4, :]]
        ones_mean = cmb.t[:, 7, :]
        ones_rms = cmb.t[:, 8, :]
        ones_row = cmb.t[0:1, 9, :]
        CONSTS = cmb.all

        k.op(DVE, lambda: nc.vector.memset(zT.t[:], 0.0), outs=zT.all)
        k.op(DVE, lambda: nc.vector.memset(zT.t[32:33], 1.0), outs=zT.all)
        act(csil.t[:], cT.t[:], AF.Silu, cT.all, csil.all)
        for l in range(depth if stop != "casts" else 0):
            p = getps()
            for i in range(12):
                slot = wnext(l, S_MOD0 + i)
                wv = slot.t[:].rearrange("p (k c) -> p k c", k=KC)
                for m in range(4):
                    ch = i * 4 + m
                    for kc in range(KC):
                        mm(p.t[:, ch * 4:ch * 4 + 4], wv[:, kc, m * 128:(m + 1) * 128], csil.t[:, kc, :],
                           kc == 0, kc == KC - 1, ins=slot.all + csil.all, outs=p.all)
            pv = p.t[:, 0:192].rearrange("p (c j) -> p c j", j=4)
            for j in range(3):
                tt(DVE, mod.t[:, l, j, :], pv[:, :, j], vec.t[:, l, V_BMOD:V_BMOD + 48], ALU.add,
                   ins=p.all + vec.all, outs=mod.all)
            for j in range(3):
                ts(DVE, drv.t[:, l, j, 0:8], mod.t[:, l, j, 8:16], 1.0, 1.0 / ALPHA, ALU.add, ALU.mult,
                   mod.all, drv.all, hazard=True)
                ts(DVE, drv.t[:, l, j, 8:16], mod.t[:, l, j, 32:40], 1.0, 1.0 / ALPHA, ALU.add, ALU.mult,
                   mod.all, drv.all, hazard=True)
            last = (l == depth - 1)
            ts(DVE, drv2.t[:, l, 0:16], vec.t[:, l, V_LN1W:V_LN1W + 16], ALPHA, None, ALU.mult, None, vec.all, drv2.all)
            ts(DVE, drv2.t[:, l, 16:32], vec.t[:, l, V_LN2W:V_LN2W + 16], 1.0 if (last and (depth == DEPTH or force_final)) else ALPHA, None,
               ALU.mult, None, vec.all, drv2.all)
        MODS = mod.all + drv.all + drv2.all + vec.all

        def layer_norm(xt, lw, lb, defer=False):
            for kc in range(KC):
                cp(DVE, ybf.t[:, kc, :], xt.t[:, kc, :], [xt.b[kc]], [ybf.b[kc]])
                act(ysq.t[:, kc, :], xt.t[:, kc, :], AF.Square, [xt.b[kc]], [ysq.b[kc]])
            if defer:
                return lambda: layer_norm_b(xt, lw, lb)
            layer_norm_b(xt, lw, lb)

        def layer_norm_b(xt, lw, lb):
            pm, pq = getps(), getps()
            for kc in range(KC):
                mm(pm.t[:, 0:T], ones_mean, ybf.t[:, kc, :], kc == 0, kc == KC - 1, CONSTS + [ybf.b[kc]], pm.all)
            for kc in range(KC):
                mm(pq.t[:, 0:T], ones_mean, ysq.t[:, kc, :], kc == 0, kc == KC - 1, CONSTS + [ysq.b[kc]], pq.all)
            cp(ACT, lnm.t[:], pm.t[:, 0:T], pm.all, lnm.all)
            act(lnm2.t[:], pm.t[:, 0:T], AF.Square, pm.all, lnm2.all)
            tt(DVE, lnv.t[:], pq.t[:, 0:T], lnm2.t[:], ALU.subtract, pq.all + lnm2.all, lnv.all)
            ts(DVE, lnv.t[:], lnv.t[:], EPS, None, ALU.add, None, lnv.all, lnv.all)
            act(lnr.t[:], lnv.t[:], AF.Ln, lnv.all, lnr.all)
            act(lnr.t[:], lnr.t[:], AF.Exp, lnr.all, lnr.all, scale=-0.5)
            for kc in range(KC):
                tt(DVE, lnt.t[:, kc, :], xt.t[:, kc, :], lnm.t[:], ALU.subtract, [xt.b[kc]] + lnm.all, [lnt.b[kc]])
                sc = lw if isinstance(lw, float) else lw(kc)
                xin_ = [lnt.b[kc]] + lnr.all + ([] if isinstance(lw, float) else MODS)
                if lb is None:
                    stt(DVE, xt.t[:, kc, :], lnt.t[:, kc, :], sc, lnr.t[:], ALU.mult, ALU.mult, xin_, [xt.b[kc]])
                else:
                    stt(DVE, lnt.t[:, kc, :], lnt.t[:, kc, :], sc, lnr.t[:], ALU.mult, ALU.mult, xin_, [lnt.b[kc]])
                    act(xt.t[:, kc, :], lnt.t[:, kc, :], AF.Identity, [lnt.b[kc]] + MODS, [xt.b[kc]], bias=lb(kc))

        def modulate(xt, l, j, which, dst=None):
            ud = u.tiles[u.i if dst is None else dst]
            for kc in range(KC):
                su = drv.t[:, l, j, which * 8 + kc:which * 8 + kc + 1]
                sh = mod.t[:, l, j, which * 24 + kc:which * 24 + kc + 1]
                if kc % 2 == 0:
                    act(ud.t[:, kc, :], xt.t[:, kc, :], AF.Identity, [xt.b[kc]] + MODS, [ud.b[kc]], scale=su, bias=sh)
                else:
                    ts(DVE, ud.t[:, kc, :], xt.t[:, kc, :], su, sh, ALU.mult, ALU.add, [xt.b[kc]] + MODS, [ud.b[kc]])

        def proj_fm(l, s, nout, evac):
            slot = wnext(l, s)
            wv = slot.t[:].rearrange("p (k c) -> p k c", k=KC)
            per = 512 // T
            m = 0
            while m < nout:
                p = getps()
                n_here = min(per, nout - m)
                for q in range(n_here):
                    for kc in range(KC):
                        mm(p.t[:, q * T:(q + 1) * T], wv[:, kc, (m + q) * 128:(m + q + 1) * 128], u.t[:, kc, :],
                           kc == 0, kc == KC - 1, slot.all + [u.b[kc]], p.all)
                evac(m, n_here, p)
                m += n_here

        def proj_tm(l, s, dst, col0):
            slot = wnext(l, s)
            wv = slot.t[:].rearrange("p (k c) -> p k c", k=KC)
            for c in range(NCH):
                p = getps()
                for kc in range(KC):
                    mm(p.t[:, :], u.t[:, kc, c * 128:(c + 1) * 128], wv[:, kc, :], kc == 0, kc == KC - 1,
                       slot.all + [u.b[kc]], p.all)
                cp(ACT if c % 2 == 0 else DVE, dst.t[:, c, col0:col0 + 512], p.t[:, :], p.all, [dst.b[c]])

        def gla_stage1(l, c, dr, par):
            cs = slice(c * 128, (c + 1) * 128)
            wg = lwt.t[0:33, l, 512:1024] if dr == 0 else wgbt.t[0:33, l, :]
            p = getps()
            mm(p.t[:, :], zT.t[:, dr, cs], wg, True, True, zT.all + lwt.all + wgbt.all, p.all)
            act(e1[par].t[:], p.t[:, :], AF.Exp, p.all, e1[par].all, scale=-1.0)
            act(sp_[par].t[:], e1[par].t[:], AF.Ln, e1[par].all, sp_[par].all, bias=1.0)
            ck('s1a')
            p2 = getps()
            for h in range(HEADS):
                mm(p2.t[:, h * 128:(h + 1) * 128], sp_[par].t[:, h * 128:(h + 1) * 128], Umat[dr], True, True,
                   sp_[par].all + CONSTS, p2.all, sig=(h == HEADS - 1))
            p2v = p2.t[:, :].rearrange("p (h i) -> p h i", h=HEADS)
            act(Ep[par].t[:], p2v, AF.Exp, p2.all, Ep[par].all)
            act(Em[par].t[:], p2v, AF.Exp, p2.all, Em[par].all, scale=-1.0)
            ck('s1b')
            tt(DVE, qe[par].t[:], qT.t[:, :, cs], Ep[par].t[:], ALU.mult, qT.all + Ep[par].all, qe[par].all)
            tt(DVE, ke[par].t[:], kT.t[:, :, cs], Em[par].t[:], ALU.mult, kT.all + Em[par].all, ke[par].all)
            ck('s1c')
            p3 = getps()
            mm(p3.t[:, :], W2mat[dr], sp_[par].t[:], True, True, sp_[par].all + CONSTS, p3.all)
            act(er[par].t[:], p3.t[:, :], AF.Exp, p3.all, er[par].all)
            ck('s1d')
            pb = pst_t[:, par * 512:(par + 1) * 512]
            for h in range(HEADS):
                k.op(PE, lambda h=h: nc.tensor.transpose(out=pst_t[:, par * 512 + h * 128:par * 512 + (h + 1) * 128],
                                                          in_=kT.t[:, h, cs], identity=identb),
                     ins=kT.all + CONSTS, outs=[pst[par]], sig=(h == HEADS - 1))
            tt(DVE, kend[par].t[:], pb, er[par].t[:], ALU.mult, [pst[par]] + er[par].all, kend[par].all)
            ck('s1e')
            p4 = getps()
            for h in range(HEADS):
                mm(p4.t[:, h * 128:(h + 1) * 128], ke[par].t[:, h, :], qe[par].t[:, h, :], True, True,
                   ke[par].all + qe[par].all, p4.all, sig=(h == HEADS - 1))
            tt(DVE, attm[par].t[:], p4.t[:, :].rearrange("p (h i) -> p h i", h=HEADS), maskr.t[:, dr], ALU.mult,
               p4.all + maskr.all, attm[par].all)
            ck('s1f')

        def gla_stage2(c, dr, par, omode):
            cs = slice(c * 128, (c + 1) * 128)
            S, Sb_ = Sst[dr], Sbf[dr]
            if omode is not None:
                for half in range(2):
                    p = getps()
                    for q in range(4):
                        hv = half * 4 + q
                        h, vc = hv // 2, hv % 2
                        mm(p.t[:, q * 128:(q + 1) * 128], vtm.t[:, c, h * 256 + vc * 128:h * 256 + (vc + 1) * 128],
                           attm[par].t[:, h, :], True, False, [vtm.b[c]] + attm[par].all, p.all, sig=False)
                        mm(p.t[:, q * 128:(q + 1) * 128], Sb_.t[:, h, vc * 128:(vc + 1) * 128], qe[par].t[:, h, :],
                           False, True, Sb_.all + qe[par].all, p.all, sig=(q == 3))
                    ck('s2m')
                    pv = p.t[:, :].rearrange("p (q i) -> p q i", q=4)
                    ov = o.t[:, half * 4:(half + 1) * 4, cs]
                    if omode == "copy":
                        cp(ACT, ov, pv, p.all, o.all)
                    else:
                        tt(DVE, ov, pv, ov, ALU.add, p.all + o.all, o.all)
                ck('s2a')
            last = 127 if dr == 0 else 0
            for half in range(2):
                p = getps()
                for q in range(2):
                    h = half * 2 + q
                    mm(p.t[:, q * 256:(q + 1) * 256], kend[par].t[:, h * 128:(h + 1) * 128], vtm.t[:, c, h * 256:(h + 1) * 256],
                       True, True, kend[par].all + [vtm.b[c]], p.all, sig=(q == 1))
                ck('s2d')
                for q in range(2):
                    h = half * 2 + q
                    stt(DVE, S.t[:, h, :], S.t[:, h, :], Ep[par].t[:, h, last:last + 1], p.t[:, q * 256:(q + 1) * 256],
                        ALU.mult, ALU.add, S.all + Ep[par].all + p.all, S.all)
                ck('s2s')
            cp(ACT, Sb_.t[:], S.t[:], S.all, Sb_.all)
            ck('s2c')

        def gla_scan(l, dr, omode, fill=()):
            fill = list(fill)

            def filler():
                if fill:
                    fill.pop(0)()
            order = list(range(NCH)) if dr == 0 else list(range(NCH - 1, -1, -1))
            gla_stage1(l, order[0], dr, 0)
            filler()
            for n, c in enumerate(order):
                if n + 1 < len(order):
                    gla_stage1(l, order[n + 1], dr, (n + 1) % 2)
                    filler()
                gla_stage2(c, dr, n % 2, omode)
                filler()
            while fill:
                filler()

        def proj_gla_inputs(l):
            def ev_z(m, n, p):
                cp(ACT, zT.t[0:16, :, :], p.t[0:16, 0:2 * T].rearrange("p (a t) -> p a t", a=2), p.all, zT.all)
            slot = wnext(l, S_Z)
            wv = slot.t[:].rearrange("p (k c) -> p k c", k=KC)
            p = getps()
            for zz in range(2):
                for kc in range(KC):
                    mm(p.t[0:16, zz * T:(zz + 1) * T], wv[:, kc, zz * 32:zz * 32 + 16], u.t[:, kc, :], kc == 0, kc == KC - 1,
                       slot.all + [u.b[kc]], p.all)
            ev_z(0, 1, p)

            def ev_q(m, n, p):
                k.op(ACT, lambda: nc.scalar.mul(out=qT.t[:, m:m + n, :], in_=p.t[:, 0:n * T].rearrange("p (a t) -> p a t", a=n),
                                                mul=QSCALE), ins=p.all, outs=qT.all)
            proj_fm(l, S_Q, 4, ev_q)

            def ev_k(m, n, p):
                cp(DVE, kT.t[:, m:m + n, :], p.t[:, 0:n * T].rearrange("p (a t) -> p a t", a=n), p.all, kT.all)
            proj_fm(l, S_K, 4, ev_k)

        def proj_v(l):
            proj_tm(l, S_V0, vtm, 0)
            proj_tm(l, S_V1, vtm, 512)

        def load_xa(b, t, par):
            k.dma(POOL, f"xa{par}", xa[par].t[:], xs_d[b, t], ins=[xsb[b][t]], outs=xa[par].all)

        xsb = [[Buf(f"xs{b}_{t}") for t in range(NT + 1)] for b in range(NBC)]
        obb = [[Buf(f"ob{b}_{t}") for t in range(NT)] for b in range(NBC)]
        posb = Buf("posd")
        xind = Buf("xind")
        post = otmp

        par = 0
        for b in range(NBC if stop not in ("prologue", "casts") else 0):
            for t in range(NT + 1):
                X = xa[par]
                if t < NT:
                    src = x_d[b, t * T:(t + 1) * T, :].rearrange("(c p) d -> p c d", p=128)
                    k.dma(POOL, "pos", post.t[:], pos_d[t], ins=[posb], outs=post.all)
                else:
                    src = ctx_d[b].rearrange("(c p) d -> p c d", p=128)
                k.dma(POOL, "xin", xin.t[:], src, ins=[xind], outs=xin.all)
                for kc in range(0, KC, 512 // T):
                    p = getps()
                    nk = 512 // T
                    for q in range(nk):
                        for c in range(NCH):
                            k.op(PE, lambda q=q, c=c, kc=kc, p=p: nc.tensor.transpose(
                                out=p.t[:, q * T + c * 128:q * T + (c + 1) * 128],
                                in_=xin.t[:, c, (kc + q) * 128:(kc + q + 1) * 128], identity=ident32),
                                ins=xin.all + cm32.all, outs=p.all, sig=(q == nk - 1 and c == NCH - 1))
                    pv = p.t[:, 0:nk * T].rearrange("p (a t) -> p a t", a=nk)
                    if t < NT:
                        tt(DVE, X.t[:, kc:kc + nk, :], pv, post.t[:, kc:kc + nk, :], ALU.add, p.all + post.all,
                           X.b[kc:kc + nk])
                    else:
                        cp(DVE, X.t[:, kc:kc + nk, :], pv, p.all, X.b[kc:kc + nk])
                layer_norm(X, ALPHA, None)
                k.dma(POOL, f"xst{par}", xs_d[b, t], X.t[:], ins=X.all, outs=[xsb[b][t]])
                par ^= 1

        try:
            sched = [] if stop in ("none", "casts", "prologue", "phase0") else list(full_sched)
            if ntiles is not None:
                sched = sched[:ntiles]
            pending = [None]
            zqk_done = [False]

            def run_pending():
                if pending[0] is not None:
                    f = pending[0]
                    pending[0] = None
                    f()

            def ev_r(base, dst, func):
                def f(m, n, p):
                    act(dst.t[:, base + m:base + m + n, :], p.t[:, 0:n * T].rearrange("p (a t) -> p a t", a=n), func,
                        p.all, dst.b[base + m:base + m + n])
                return f

            if sched:
                l0, b0, kind0, t0 = sched[0]
                load_xa(b0, t0, par)
                modulate(xa[par], l0, 2 if kind0 == "ctx" else b0, 0, dst=par)
            for ti, (l, b, kind, t) in enumerate(sched):
                j = 2 if kind == "ctx" else b
                last_layer = (l == depth - 1)
                X = xa[par]
                u.i = par
                nxt = full_sched[ti + 1] if ti + 1 < len(full_sched) else None

                def after_first_s1(l=l, nxt=nxt, par=par, kind=kind, b=b, t=t):
                    proj_v(l)
                    run_pending()
                    if kind == "fwd":
                        k.dma(POOL, "ob", o.t[:], ob_d[b, t], ins=[obb[b][t]], outs=o.all)
                    if nxt is not None:
                        load_xa(nxt[1], nxt[3], par ^ 1)

                def hoist(nxt=nxt, par=par):
                    if nxt is None:
                        return
                    modulate(xa[par ^ 1], nxt[0], 2 if nxt[2] == "ctx" else nxt[1], 0, dst=par ^ 1)
                    u.i = par ^ 1
                    proj_gla_inputs(nxt[0])
                    u.i = par
                    zqk_done[0] = True
                ck('mod')
                if kind == "ctx":
                    k.op(DVE, lambda: nc.vector.memset(Sst[0].t[:], 0.0), outs=Sst[0].all)
                    k.op(DVE, lambda: nc.vector.memset(Sst[1].t[:], 0.0), outs=Sst[1].all)
                    k.op(DVE, lambda: nc.vector.memset(Sbf[0].t[:], 0.0), outs=Sbf[0].all)
                    k.op(DVE, lambda: nc.vector.memset(Sbf[1].t[:], 0.0), outs=Sbf[1].all)
                if not zqk_done[0]:
                    proj_gla_inputs(l)
                zqk_done[0] = False
                ck('proj')
                if kind == "bwd":
                    gla_scan(l, 1, "copy", fill=[after_first_s1, hoist])
                    k.dma(POOL, "ost", ob_d[b, t], o.t[:], ins=o.all, outs=[obb[b][t]])
                    par ^= 1
                    continue
                if kind == "ctx":
                    if last_layer:
                        gla_scan(l, 0, None, fill=[after_first_s1])
                        gla_scan(l, 1, None, fill=[lambda: None, hoist])
                        par ^= 1
                        continue
                    gla_scan(l, 0, "copy", fill=[after_first_s1])
                    ck('gla1')
                    gla_scan(l, 1, "add")
                    ck('gla2')
                    proj_tm(l, S_POOL, ptm, 0)
                    proj_fm(l, S_R0, 4, ev_r(0, rs, AF.Silu))
                    proj_fm(l, S_R0 + 1, 4, ev_r(4, rs, AF.Silu))
                else:
                    def f1():
                        proj_tm(l, S_POOL, ptm, 0)
                        proj_fm(l, S_R0, 4, ev_r(0, rs, AF.Silu))

                    def f2():
                        proj_fm(l, S_R0 + 1, 4, ev_r(4, rs, AF.Silu))
                    gla_scan(l, 0, "add", fill=[after_first_s1, f1, f2])

                ck('inproj')
                act(osq.t[:], o.t[:], AF.Square, o.all, osq.all)
                for h in range(HEADS):
                    p = getps()
                    mm(p.t[:, 0:T], ones_rms, osq.t[:, 2 * h, :], True, False, CONSTS + osq.all, p.all, sig=False)
                    mm(p.t[:, 0:T], ones_rms, osq.t[:, 2 * h + 1, :], False, True, CONSTS + osq.all, p.all)
                    ts(DVE, rstd.t[:, h, :], p.t[:, 0:T], EPS, None, ALU.add, None, p.all, rstd.all)
                act(rstd.t[:], rstd.t[:], AF.Ln, rstd.all, rstd.all)
                act(rstd.t[:], rstd.t[:], AF.Exp, rstd.all, rstd.all, scale=-0.5)
                for hv in range(KC):
                    tt(DVE, otmp.t[:, hv, :], o.t[:, hv, :], rstd.t[:, hv // 2, :], ALU.mult, o.all + rstd.all, [otmp.b[hv]])
                    stt(DVE, of.t[:, hv, :], otmp.t[:, hv, :], vec.t[:, l, V_GNW + hv:V_GNW + hv + 1],
                        rs.t[:, hv, :], ALU.mult, ALU.mult, [otmp.b[hv], rs.b[hv]] + vec.all, [of.b[hv]])

                ck('rms')
                for g in range(4):
                    p = getps()
                    for tc in range(NCH):
                        if kind == "ctx":
                            for jc in range(NCH):
                                mm(p.t[:, tc * 128:(tc + 1) * 128], ptm.t[:, jc, g * 128:(g + 1) * 128],
                                   cmb.t[:, 14 + g * 4 + jc * 2 + tc, :], jc == 0, jc == NCH - 1, [ptm.b[jc]] + CONSTS, p.all,
                                   sig=(jc == NCH - 1 and tc == NCH - 1))
                        else:
                            mm(p.t[:, tc * 128:(tc + 1) * 128], ptm.t[:, tc, g * 128:(g + 1) * 128], cmb.t[:, 10 + g, :],
                               True, True, [ptm.b[tc]] + CONSTS, p.all, sig=(tc == NCH - 1))
                    cp(ACT, pooledT.t[:, g, :], p.t[:, 0:T], p.all, [pooledT.b[g]])
                proj_fm(l, S_GP0, 4, ev_r(0, gp, AF.Sigmoid))
                proj_fm(l, S_GP0 + 1, 4, ev_r(4, gp, AF.Sigmoid))
                for g in range(4):
                    p = getps()
                    mm(p.t[:, 0:T], lwt.t[:, l, g * 128:(g + 1) * 128], pooledT.t[:, g, :], True, True,
                       lwt.all + [pooledT.b[g]], p.all)
                    ts(DVE, pooled2.t[:, g, :], p.t[:, 0:T], vec.t[:, l, V_PSC + g:V_PSC + g + 1], None, ALU.mult, None,
                       p.all + vec.all, [pooled2.b[g]])
                proj_fm(l, S_GG0, 4, ev_r(0, gg, AF.Sigmoid))
                proj_fm(l, S_GG0 + 1, 4, ev_r(4, gg, AF.Sigmoid))
                slot = wnext(l, S_BRPOOL)
                wv = slot.t[:].rearrange("p (g c) -> p g c", g=4)
                for m in range(KC):
                    p = getps()
                    for g in range(4):
                        mm(p.t[:, 0:T], wv[:, g, m * 128:(m + 1) * 128], pooled2.t[:, g, :], g == 0, g == 3,
                           slot.all + [pooled2.b[g]], p.all)
                    tt(DVE, t1.t[:, m, :], p.t[:, 0:T], gp.t[:, m, :], ALU.mult, p.all + [gp.b[m]], [t1.b[m]])

                ck('pool')
                def gen_fm(l, s0, src, evac):
                    for sidx in range(2):
                        slot = wnext(l, s0 + sidx)
                        wv = slot.t[:].rearrange("p (k c) -> p k c", k=KC)
                        for mq in range(4):
                            m = sidx * 4 + mq
                            p = getps()
                            for kc in range(KC):
                                mm(p.t[:, 0:T], wv[:, kc, mq * 128:(mq + 1) * 128], src.t[:, kc, :], kc == 0, kc == KC - 1,
                                   slot.all + [src.b[kc]], p.all)
                            evac(m, p)

                def ev_gla(m, p):
                    tt(DVE, t2.t[:, m, :], p.t[:, 0:T], gg.t[:, m, :], ALU.mult, p.all + [gg.b[m]], [t2.b[m]])
                    tt(POOL, merged.t[:, m, :], t1.t[:, m, :], t2.t[:, m, :], ALU.add, [t1.b[m], t2.b[m]], [merged.b[m]])
                gen_fm(l, S_BRGLA0, of, ev_gla)

                def ev_mix(m, p):
                    stt(DVE, X.t[:, m, :], p.t[:, 0:T], mod.t[:, l, j, 16 + m:17 + m], X.t[:, m, :], ALU.mult, ALU.add,
                        p.all + [X.b[m]] + MODS, [X.b[m]])
                gen_fm(l, S_OUT0, merged, ev_mix)
                ck('mix')
                lnb1 = layer_norm(X, lambda kc: drv2.t[:, l, kc:kc + 1], lambda kc: drv2.t[:, l, 8 + kc:9 + kc], defer=True)
                hoist()
                lnb1()
                ck('ln1')
                modulate(X, l, j, 1)

                rows, rl = (T // 64, 64) if kind == "fwd" else (1, T)
                for s in range(11):
                    slot = wnext(l, S_UP0 + s)
                    wv = slot.t[:].rearrange("p (k c) -> p k c", k=KC)
                    bp = s % 2
                    pa, pg = getps(), getps()
                    for q in range(4):
                        p = pa if q < 2 else pg
                        for kc in range(KC):
                            mm(p.t[:, (q % 2) * T:(q % 2 + 1) * T], wv[:, kc, q * 128:(q + 1) * 128], u.t[:, kc, :], kc == 0,
                               kc == KC - 1, slot.all + [u.b[kc]], p.all)
                    A, C_, G = asb[bp], acc[bp], gel[bp]
                    cp(ACT, A.t[:], pa.t[:, 0:2 * T].rearrange("p (a t) -> p a t", a=2), pa.all, A.all)
                    for q in range(2):
                        fc = 2 * s + q
                        act(C_.t[:, q, :], pa.t[:, q * T:(q + 1) * T], AF.Identity, pa.all + vec.all, C_.all,
                            scale=vec.t[:, l, V_CW + 22 + fc:V_CW + 23 + fc], bias=vec.t[:, l, V_CB + fc:V_CB + fc + 1])
                        av = A.t[:, q, :].rearrange("p (r w) -> p r w", w=rl)
                        cv = C_.t[:, q, :].rearrange("p (r w) -> p r w", w=rl)
                        stt(DVE, cv[:, :, 1:rl], av[:, :, 0:rl - 1], vec.t[:, l, V_CW + fc:V_CW + fc + 1], cv[:, :, 1:rl],
                            ALU.mult, ALU.add, A.all + C_.all + vec.all, C_.all)
                        stt(DVE, cv[:, :, 0:rl - 1], av[:, :, 1:rl], vec.t[:, l, V_CW + 44 + fc:V_CW + 45 + fc],
                            cv[:, :, 0:rl - 1], ALU.mult, ALU.add, A.all + C_.all + vec.all, C_.all)
                    act(G.t[:], C_.t[:], AF.Gelu, C_.all, G.all)
                    hT = (rs, gp, gg)[(2 * s) // 8]
                    hi = (2 * s) % 8
                    tt(DVE, hT.t[:, hi:hi + 2, :], pg.t[:, 0:2 * T].rearrange("p (a t) -> p a t", a=2), G.t[:], ALU.mult,
                       pg.all + G.all, hT.b[hi:hi + 2])
                ck('up')
                per = 512 // T
                pacc = [getps() for _ in range(KC // per)]
                if per > 1:
                    for p in pacc:
                        mm(p.t[:, :], cmb.t[:, 30, :], cmb.t[:, 30:34, :], True, True, CONSTS, p.all)
                for s in range(6):
                    slot = wnext(l, S_DOWN0 + s)
                    wv = slot.t[:].rearrange("p (k c) -> p k c", k=4)
                    nk = 4 if s < 5 else 2
                    for m in range(KC):
                        p = pacc[m // per]
                        for q in range(nk):
                            fc = 4 * s + q
                            hT = (rs, gp, gg)[fc // 8]
                            mm(p.t[:, (m % per) * T:(m % per + 1) * T], wv[:, q, m * 128:(m + 1) * 128], hT.t[:, fc % 8, :],
                               (fc == 0 and per == 1), fc == FC - 1, slot.all + [hT.b[fc % 8]], p.all, sig=(q == nk - 1))
                for m in range(KC):
                    p = pacc[m // per]
                    stt(DVE, X.t[:, m, :], p.t[:, (m % per) * T:(m % per + 1) * T], mod.t[:, l, j, 40 + m:41 + m], X.t[:, m, :],
                        ALU.mult, ALU.add, p.all + [X.b[m]] + MODS, [X.b[m]])
                lnb = layer_norm(X, lambda kc, l=l: drv2.t[:, l, 16 + kc:17 + kc],
                                 (lambda kc, l=l: drv2.t[:, l, 24 + kc:25 + kc]), defer=True)
                is_final = last_layer and (depth == DEPTH or force_final) and kind == "fwd"

                def tail(lnb=lnb, X=X, b=b, t=t, par=par, is_final=is_final):
                    lnb()
                    if is_final:
                        for c in range(NCH):
                            for kc0 in range(0, KC, 4):
                                p = getps()
                                for q in range(4):
                                    kc = kc0 + q
                                    k.op(PE, lambda q=q, kc=kc, c=c, p=p: nc.tensor.transpose(
                                        out=p.t[:, q * 128:(q + 1) * 128], in_=X.t[:, kc, c * 128:(c + 1) * 128], identity=ident32),
                                        ins=[X.b[kc]] + cm32.all, outs=p.all, sig=(q == 3))
                                cp(ACT if kc0 else DVE, xout.t[:, c, kc0 * 128:(kc0 + 4) * 128], p.t[:, :], p.all, xout.all)
                        dst = out_d[b, t * T:(t + 1) * T, :].rearrange("(c p) d -> p c d", p=128)
                        k.dma(POOL, "outst", dst, xout.t[:], ins=xout.all, outs=[Buf("outd")])
                    else:
                        k.dma(POOL, f"xst{par}", xs_d[b, t], X.t[:], ins=X.all, outs=[xsb[b][t]])
                pending[0] = tail
                par ^= 1
            run_pending()
        except StopBuild:
            pass
        assert stop or ntiles is not None or wstate["next"] == len(plan), (wstate, len(plan))
        for name, ds in k.dsem.items():
            if ds.cnt:
                k.wait(SP, (ds, ds.cnt))
        for E in (PE, ACT, DVE, POOL):
            if E.cnt:
                k.wait(SP, (E, E.cnt))
    return nc


def _pos_embed():
    rows, cols, dim = L // 64, 64, D
    quarter = dim // 4
    omega = (1.0 / (10000.0 ** (np.arange(quarter, dtype=np.float32) / np.float32(quarter)))).astype(np.float32)
    r = np.arange(rows, dtype=np.float32)[:, None] * omega
    cl = np.arange(cols, dtype=np.float32)[:, None] * omega
    er = np.concatenate([np.sin(r), np.cos(r)], -1).astype(np.float32)
    ec = np.concatenate([np.sin(cl), np.cos(cl)], -1).astype(np.float32)
    emb = np.concatenate([np.broadcast_to(er[:, None, :], (rows, cols, dim // 2)),
                          np.broadcast_to(ec[None, :, :], (rows, cols, dim // 2))], -1)
    return emb.reshape(rows * cols, dim).astype(np.float32)


def _pool_matrix(n, w):
    P = np.zeros((n, n), np.float32)
    for t in range(n):
        lo, hi = max(t - w // 2, 0), min(t + w // 2, n)
        P[t, lo:hi] = 1.0 / (hi - lo)
        P[t, t] -= 1.0
    return P


def _const_mats():
    cm = np.zeros((34, 128, 128), np.float32)
    j = np.arange(128)[:, None]
    i = np.arange(128)[None, :]
    cm[0] = (j == i)
    cm[1] = (j <= i) * (-1.0 / 16.0)
    cm[2] = (j >= i) * (-1.0 / 16.0)
    cm[3] = (j > i) * (-1.0 / 16.0)
    cm[4] = (j < i) * (-1.0 / 16.0)
    cm[5] = (j <= i)
    cm[6] = (j >= i)
    cm[7] = 1.0 / 1024.0
    cm[8] = 1.0 / 256.0
    cm[9] = 1.0
    for g, w in enumerate((2, 4, 8, 16)):
        P64 = _pool_matrix(64, w)
        Pg = np.zeros((128, 128), np.float32)
        Pg[0:64, 0:64] = P64
        Pg[64:128, 64:128] = P64
        cm[10 + g] = Pg.T
        Pc = _pool_matrix(256, w).T
        for jc in range(2):
            for tc in range(2):
                cm[14 + g * 4 + jc * 2 + tc] = Pc[jc * 128:(jc + 1) * 128, tc * 128:(tc + 1) * 128]
    return np.ascontiguousarray(cm.transpose(1, 0, 2))


def _k1024_slab(W):
    return np.ascontiguousarray(W.reshape(KC, 128, 512).transpose(1, 0, 2)).reshape(128, SLABW)


def _layer_slabs(w_in, w_gate_f, b_gate_f, w_gate_b, b_gate_b, w_pool, w_br_pool, w_br_gla, w_out, w_up, w_down, w_mod):
    S = np.zeros((NSLAB, 128, SLABW), np.float32)
    o_pool, o_q, o_k, o_v, o_r, o_zf, o_zb, o_gp, o_gg = 0, 512, 1024, 1536, 2560, 3584, 3600, 3616, 4640
    zc = np.zeros((1024, 512), np.float32)
    zc[:, 0:16] = w_in[:, o_zf:o_zf + 16]
    zc[:, 32:48] = w_in[:, o_zb:o_zb + 16]
    S[S_Z] = _k1024_slab(zc)
    S[S_Q] = _k1024_slab(w_in[:, o_q:o_q + 512])
    S[S_K] = _k1024_slab(w_in[:, o_k:o_k + 512])
    S[S_V0] = _k1024_slab(w_in[:, o_v:o_v + 512])
    S[S_V1] = _k1024_slab(w_in[:, o_v + 512:o_v + 1024])
    S[S_POOL] = _k1024_slab(w_in[:, o_pool:o_pool + 512])
    for i in range(2):
        S[S_R0 + i] = _k1024_slab(w_in[:, o_r + 512 * i:o_r + 512 * (i + 1)])
        S[S_GP0 + i] = _k1024_slab(w_in[:, o_gp + 512 * i:o_gp + 512 * (i + 1)])
        S[S_GG0 + i] = _k1024_slab(w_in[:, o_gg + 512 * i:o_gg + 512 * (i + 1)])
        S[S_BRGLA0 + i] = _k1024_slab(w_br_gla[:, 512 * i:512 * (i + 1)])
        S[S_OUT0 + i] = _k1024_slab(w_out[:, 512 * i:512 * (i + 1)])
    S[S_BRPOOL] = np.ascontiguousarray(w_br_pool.reshape(4, 128, 1024).transpose(1, 0, 2)).reshape(128, SLABW)
    F = 128 * FC
    for s in range(11):
        Wc = np.concatenate([w_up[:, s * 256:(s + 1) * 256], w_up[:, F + s * 256:F + (s + 1) * 256]], axis=1)
        S[S_UP0 + s] = _k1024_slab(Wc)
    wd = np.zeros((24 * 128, 1024), np.float32)
    wd[:F] = w_down
    for s in range(6):
        S[S_DOWN0 + s] = np.ascontiguousarray(wd[s * 512:(s + 1) * 512].reshape(4, 128, 1024).transpose(1, 0, 2)).reshape(128, SLABW)
    misc = np.zeros((128, SLABW), np.float32)
    misc[:, 0:512] = w_pool.transpose(1, 0, 2).reshape(128, 512)
    misc[0:16, 512:1024] = w_gate_f
    misc[0:16, 2048:2560] = w_gate_b
    misc[32, 512:1024] = b_gate_f
    misc[32, 2048:2560] = b_gate_b
    S[S_MISC] = misc
    for i in range(12):
        S[S_MOD0 + i] = _k1024_slab(w_mod[:, 512 * i:512 * (i + 1)])
    return S


def _chunkvec(v):
    return np.ascontiguousarray(v.reshape(-1, 128).T)


def _layer_vec(b_mod, ln1_w, ln1_b, ln2_w, ln2_b, gla_norm_w, pool_scale, conv_w, conv_b):
    V = np.zeros((128, NVEC), np.float32)
    V[:, V_BMOD:V_BMOD + 48] = _chunkvec(b_mod)
    V[:, V_LN1W:V_LN1W + 8] = _chunkvec(ln1_w)
    V[:, V_LN1B:V_LN1B + 8] = _chunkvec(ln1_b)
    V[:, V_LN2W:V_LN2W + 8] = _chunkvec(ln2_w)
    V[:, V_LN2B:V_LN2B + 8] = _chunkvec(ln2_b)
    V[:, V_GNW:V_GNW + 8] = _chunkvec(gla_norm_w)
    V[:, V_PSC:V_PSC + 4] = _chunkvec(pool_scale)
    for kk in range(3):
        V[:, V_CW + 22 * kk:V_CW + 22 * (kk + 1)] = _chunkvec(conv_w[kk])
    V[:, V_CB:V_CB + 22] = _chunkvec(conv_b)
    return V


def prepare_inputs(inp, depth=DEPTH, ncore=NCORE):
    f = lambda a: np.asarray(a, dtype=np.float32)
    slabs = np.stack([_layer_slabs(f(inp["w_in"][l]), f(inp["w_gate_f"][l]), f(inp["b_gate_f"][l]), f(inp["w_gate_b"][l]),
                                   f(inp["b_gate_b"][l]), f(inp["w_pool"][l]), f(inp["w_br_pool"][l]), f(inp["w_br_gla"][l]),
                                   f(inp["w_out"][l]), f(inp["w_up"][l]), f(inp["w_down"][l]), f(inp["w_mod"][l]))
                      for l in range(depth)])
    vec = np.stack([_layer_vec(f(inp["b_mod"][l]), f(inp["ln1_w"][l]), f(inp["ln1_b"][l]), f(inp["ln2_w"][l]),
                               f(inp["ln2_b"][l]), f(inp["gla_norm_w"][l]), f(inp["pool_scale"][l]), f(inp["conv_w"][l]),
                               f(inp["conv_b"][l])) for l in range(depth)], axis=1)
    vec = np.ascontiguousarray(vec)
    pos = _pos_embed()
    posT = np.ascontiguousarray(pos.reshape(NT, T, KC, 128).transpose(0, 3, 2, 1))
    cm = _const_mats()
    x, c, ctx, c_ctx = f(inp["x"]), f(inp["c"]), f(inp["ctx"]), f(inp["c_ctx"])
    maps = []
    for i in range(ncore):
        cc = np.stack([c[NBC * i], c[NBC * i + 1], c_ctx, np.zeros_like(c_ctx)], axis=0)
        cT = np.ascontiguousarray(cc.reshape(4, KC, 128).transpose(2, 1, 0))
        maps.append({"x": np.ascontiguousarray(x[NBC * i:NBC * (i + 1)]),
                     "ctx": np.ascontiguousarray(ctx[NBC * i:NBC * (i + 1)]),
                     "cT": cT, "posT": posT, "wslab": slabs, "vec": vec, "cmat": cm})
    return maps


_NC_CACHE = {}


def kernel(**inputs):
    maps = prepare_inputs(inputs)
    if "nc" not in _NC_CACHE:
        _NC_CACHE["nc"] = build_program()
    nc = _NC_CACHE["nc"]
    res = run_bass_kernel_spmd(nc, maps, core_ids=list(range(NCORE)))
    return np.concatenate([r["out"] for r in res.results], axis=0).astype(np.float32)
```

```python
import numpy as np
import concourse.bass as bass
import concourse.mybir as mybir
from concourse.bass_utils import run_bass_kernel_spmd

F32, BF16 = mybir.dt.float32, mybir.dt.bfloat16
AF = mybir.ActivationFunctionType
ALU = mybir.AluOpType

D = 1024
L = 4096
CTXL = 256
DEPTH = 4
NBC = 2
NCORE = 8
T = 256
NCH = T // 128
NT = L // T
KC = 8
FC = 22
HEADS = 4
ALPHA = float((2.0 * DEPTH) ** 0.25)
EPS = 1e-6
QSCALE = float(128 ** -0.5)
NSLOT = 4
SLABW = 4096
S_Z, S_Q, S_K, S_V0, S_V1, S_POOL, S_R0, S_GP0, S_GG0 = 0, 1, 2, 3, 4, 5, 6, 8, 10
S_BRPOOL, S_BRGLA0, S_OUT0, S_UP0, S_DOWN0, S_MISC, S_MOD0 = 12, 13, 15, 17, 28, 34, 35
NSLAB = 47
NVEC = 180
V_BMOD, V_LN1W, V_LN1B, V_LN2W, V_LN2B, V_GNW, V_PSC, V_CW, V_CB = 0, 48, 56, 64, 72, 80, 88, 92, 158


class Buf:
    __slots__ = ("w", "r", "name")

    def __init__(self, name=""):
        self.w = None
        self.r = {}
        self.name = name


class Owner:
    def __init__(self, name, sem, h=None):
        self.name, self.sem, self.h, self.cnt, self.seen = name, sem, h, 0, {}


class K:
    def __init__(self, nc, sems):
        self.nc = nc
        self.pe = Owner("pe", sems["pe"], nc.tensor)
        self.act = Owner("act", sems["act"], nc.scalar)
        self.dve = Owner("dve", sems["dve"], nc.vector)
        self.pool = Owner("pool", sems["pool"], nc.gpsimd)
        self.sp = Owner("sp", sems["sp"], nc.sync)
        self.sems = sems
        self.dsem = {}

    def dma_owner(self, name):
        if name not in self.dsem:
            self.dsem[name] = Owner("dma_" + name, self.sems["dma_" + name])
        return self.dsem[name]

    def wait(self, E, tok, hazard=False):
        if tok is None:
            return
        o, v = tok
        if o is E and not hazard:
            return
        if E.seen.get(o, 0) >= v:
            return
        E.h.wait_ge(o.sem, v)
        E.seen[o] = v

    def deps(self, E, ins, outs, hazard=False):
        for b in ins:
            self.wait(E, b.w, hazard)
        for b in outs:
            self.wait(E, b.w, hazard)
            for t in list(b.r.values()):
                self.wait(E, t, False)

    def record(self, tok, ins, outs):
        o = tok[0]
        for b in ins:
            b.r[o] = tok
        for b in outs:
            b.w = tok
            b.r = {}

    def op(self, E, instr_fn, ins=(), outs=(), hazard=False, sig=True):
        self.deps(E, ins, outs, hazard)
        i = instr_fn()
        if sig:
            E.cnt += 1
            i.then_inc(E.sem, 1)
            tok = (E, E.cnt)
        else:
            tok = (E, E.cnt + 1)
        self.record(tok, ins, outs)
        return tok

    def dma(self, Q, dname, out, in_, ins=(), outs=()):
        self.deps(Q, ins, outs)
        ds = self.dma_owner(dname)
        i = Q.h.dma_start(out=out, in_=in_)
        ds.cnt += 16
        i.then_inc(ds.sem, 16)
        tok = (ds, ds.cnt)
        self.record(tok, ins, outs)
        return tok


class StopBuild(Exception):
    pass


class Tl:
    def __init__(self, t, nchunk=1, name=""):
        self.t = t
        self.b = [Buf(f"{name}{i}") for i in range(nchunk)]

    @property
    def all(self):
        return self.b


DMA_SEMS = ["w0", "w1", "w2", "w3", "xa0", "xa1", "ob", "pos", "xin", "xst0", "xst1", "ost", "outst",
            "const", "castm", "cast0", "cast1", "cast2", "cast3"]


def build_program(depth=DEPTH, debug=False, stop=None, ntiles=None, force_final=False):
    nc = bass.Bass("TRN2", target_bir_lowering=False)
    dk = "ExternalOutput" if debug else "Internal"
    x_d = nc.dram_tensor("x", [NBC, L, D], F32, kind="ExternalInput").ap()
    ctx_d = nc.dram_tensor("ctx", [NBC, CTXL, D], F32, kind="ExternalInput").ap()
    cT_d = nc.dram_tensor("cT", [128, KC, 4], F32, kind="ExternalInput").ap()
    pos_d = nc.dram_tensor("posT", [NT, 128, KC, T], F32, kind="ExternalInput").ap()
    ws_d = nc.dram_tensor("wslab", [depth, NSLAB, 128, SLABW], F32, kind="ExternalInput").ap()
    vec_d = nc.dram_tensor("vec", [128, depth, NVEC], F32, kind="ExternalInput").ap()
    cm_d = nc.dram_tensor("cmat", [128, 34, 128], F32, kind="ExternalInput").ap()
    out_d = nc.dram_tensor("out", [NBC, L, D], F32, kind="ExternalOutput").ap()
    wb_d = nc.dram_tensor("wbf", [depth, NSLAB, 128, SLABW], BF16, kind="Internal").ap()
    xs_d = nc.dram_tensor("xs", [NBC, NT + 1, 128, KC, T], F32, kind=dk).ap()
    ob_d = nc.dram_tensor("obs", [NBC, NT, 128, KC, T], F32, kind=dk).ap()

    import contextlib
    es = contextlib.ExitStack()
    with es:
        def sb(name, shape, dt):
            return es.enter_context(nc.sbuf_tensor("s_" + name, shape, dt))

        sems = {}
        for n in ["pe", "act", "dve", "pool", "sp"] + ["dma_" + d for d in DMA_SEMS]:
            sems[n] = es.enter_context(nc.semaphore(n))
        k = K(nc, sems)
        PE, ACT, DVE, POOL, SP = k.pe, k.act, k.dve, k.pool, k.sp

        cm32 = Tl(sb("ident32", [128, 128], F32), 1, "ident32")
        cmb = Tl(sb("cmb", [128, 34, 128], BF16), 1, "cmb")
        vec = Tl(sb("vec", [128, depth, NVEC], F32), 1, "vec")
        lwt = Tl(sb("lwt", [128, depth, 1024], BF16), 1, "lwt")
        wgbt = Tl(sb("wgbt", [33, depth, 512], BF16), 1, "wgbt")
        cT = Tl(sb("cT", [128, KC, 4], F32), 1, "cT")
        csil = Tl(sb("csil", [128, KC, 4], BF16), 1, "csil")
        mod = Tl(sb("mod", [128, depth, 3, 48], F32), 1, "mod")
        drv = Tl(sb("drv", [128, depth, 3, 16], F32), 1, "drv")
        drv2 = Tl(sb("drv2", [128, depth, 32], F32), 1, "drv2")
        maskr = Tl(sb("maskr", [128, 2, HEADS, 128], BF16), 1, "maskr")

        wring = [Tl(sb(f"wr{i}", [128, SLABW], BF16), 1, f"wr{i}") for i in range(NSLOT)]

        xa = [Tl(sb(f"xa{i}", [128, KC, T], F32), KC, f"xa{i}") for i in range(2)]
        class Cur:
            def __init__(self, tiles):
                self.tiles, self.i = tiles, 0

            @property
            def t(self):
                return self.tiles[self.i].t

            @property
            def b(self):
                return self.tiles[self.i].b

        u = Cur([Tl(sb(f"u{i}", [128, KC, T], BF16), KC, f"u{i}") for i in range(2)])
        zT = Tl(sb("zT", [33, 2, T], BF16), 1, "zT")
        qT = Tl(sb("qT", [128, HEADS, T], BF16), 1, "qT")
        kT = Tl(sb("kT", [128, HEADS, T], BF16), 1, "kT")
        vtm = Tl(sb("vtm", [128, NCH, 1024], BF16), NCH, "vtm")
        ptm = Tl(sb("ptm", [128, NCH, 512], BF16), NCH, "ptm")
        rs = Tl(sb("rs", [128, KC, T], BF16), KC, "rs")
        gp = Tl(sb("gp", [128, KC, T], BF16), KC, "gp")
        gg = Tl(sb("gg", [128, KC, T], BF16), KC, "gg")
        o_t = sb("o", [128, KC * T], F32)
        o = Tl(o_t[:].rearrange("p (k t) -> p k t", k=KC), 1, "o")
        rstd = Tl(sb("rstd", [128, HEADS, T], F32), 1, "rstd")
        otmp = Tl(sb("otmp", [128, KC, T], F32), KC, "otmp")
        pooledT = Tl(sb("pooledT", [128, 4, T], BF16), 4, "pooledT")
        pooled2 = Tl(sb("pooled2", [128, 4, T], BF16), 4, "pooled2")
        merged = Tl(sb("merged", [128, KC, T], BF16), KC, "merged")
        ybf = Tl(sb("ybf", [128, KC, T], BF16), KC, "ybf")
        ysq = Tl(sb("ysq", [128, KC, T], BF16), KC, "ysq")
        lnm = Tl(sb("lnm", [128, T], F32), 1, "lnm")
        lnm2 = Tl(sb("lnm2", [128, T], F32), 1, "lnm2")
        lnv = Tl(sb("lnv", [128, T], F32), 1, "lnv")
        lnr = Tl(sb("lnr", [128, T], F32), 1, "lnr")
        lnt = Tl(sb("lnt", [128, KC, T], F32), KC, "lnt")
        t1, t2, of = lnt, otmp, ysq
        osq = Tl(ybf.t, 1, "osq")
        osq.b = ybf.b
        asb = [Tl(sb(f"asb{i}", [128, 2, T], F32), 1, f"asb{i}") for i in range(2)]
        acc = [Tl(sb(f"acc{i}", [128, 2, T], F32), 1, f"acc{i}") for i in range(2)]
        gel = asb
        xin = Tl(o_t[:].rearrange("p (c d) -> p c d", c=NCH), 1, "xin")
        xin.b = o.b
        xout = xin
        e1_ = Tl(sb("e1", [128, 512], F32), 1, "e1")
        e1 = [e1_, e1_]
        sp_ = [Tl(sb(f"sp{i}", [128, 512], BF16), 1, f"sp{i}") for i in range(2)]
        Ep = [Tl(sb(f"Ep{i}", [128, HEADS, 128], F32), 1, f"Ep{i}") for i in range(2)]
        Em = [Tl(sb(f"Em{i}", [128, HEADS, 128], F32), 1, f"Em{i}") for i in range(2)]
        qe = [Tl(sb(f"qe{i}", [128, HEADS, 128], BF16), 1, f"qe{i}") for i in range(2)]
        ke = [Tl(sb(f"ke{i}", [128, HEADS, 128], BF16), 1, f"ke{i}") for i in range(2)]
        er = [Tl(sb(f"er{i}", [128, 512], F32), 1, f"er{i}") for i in range(2)]
        kend = [Tl(sb(f"kend{i}", [128, 512], BF16), 1, f"kend{i}") for i in range(2)]
        attm = [Tl(sb(f"attm{i}", [128, HEADS, 128], BF16), 1, f"attm{i}") for i in range(2)]
        Sst = [Tl(sb(f"S{i}", [128, HEADS, 256], F32), 1, f"S{i}") for i in range(2)]
        Sbf = [Tl(sb(f"Sbf{i}", [128, HEADS, 256], BF16), 1, f"Sbf{i}") for i in range(2)]

        psb = [Tl(es.enter_context(nc.psum_tensor(f"ps{i}", [128, 512], F32)), 1, f"ps{i}") for i in range(7)]
        pst_t = es.enter_context(nc.psum_tensor("pst", [128, 1024], BF16))
        pst = [Buf("pst0"), Buf("pst1")]
        psn = [0]

        def getps():
            p = psb[psn[0] % 7]
            psn[0] += 1
            return p

        ckcnt = {}

        def ck(name):
            if stop is None:
                return
            nm, _, occ = stop.partition(':')
            if nm == name:
                ckcnt[name] = ckcnt.get(name, 0) + 1
                if ckcnt[name] >= int(occ or 1):
                    raise StopBuild()

        def mm(out, lhsT, rhs, start, stop, ins, outs, sig=None):
            if sig is None:
                sig = stop
            return k.op(PE, lambda: nc.tensor.matmul(out, lhsT=lhsT, rhs=rhs, start=start, stop=stop),
                        ins=ins, outs=outs, sig=sig)

        def act(out, in_, func, ins, outs, scale=1.0, bias=0.0, hazard=False):
            return k.op(ACT, lambda: nc.scalar.activation(out=out, in_=in_, func=func, bias=bias, scale=scale),
                        ins=ins, outs=outs, hazard=hazard)

        def tt(E, out, in0, in1, op, ins, outs):
            return k.op(E, lambda: E.h.tensor_tensor(out=out, in0=in0, in1=in1, op=op), ins=ins, outs=outs)

        def ts(E, out, in0, s1, s2, op0, op1, ins, outs, hazard=False):
            if s2 is None:
                return k.op(E, lambda: E.h.tensor_scalar(out=out, in0=in0, scalar1=s1, scalar2=None, op0=op0),
                            ins=ins, outs=outs, hazard=hazard)
            return k.op(E, lambda: E.h.tensor_scalar(out=out, in0=in0, scalar1=s1, scalar2=s2, op0=op0, op1=op1),
                        ins=ins, outs=outs, hazard=hazard)

        def stt(E, out, in0, scalar, in1, op0, op1, ins, outs):
            return k.op(E, lambda: E.h.scalar_tensor_tensor(out=out, in0=in0, scalar=scalar, in1=in1, op0=op0, op1=op1),
                        ins=ins, outs=outs)

        def cp(E, out, in_, ins, outs):
            if E is ACT:
                return act(out, in_, AF.Copy, ins, outs)
            return k.op(E, lambda: E.h.tensor_copy(out=out, in_=in_), ins=ins, outs=outs)

        plan = []
        wstate = {"issued": 0, "next": 0}
        wbuf_d = [[Buf(f"wbf{l}_{s}") for s in range(NSLAB)] for l in range(depth)]

        def tiles_schedule():
            for l in range(depth):
                for b in range(NBC):
                    yield (l, b, "ctx", NT)
                    for t in range(NT - 1, -1, -1):
                        yield (l, b, "bwd", t)
                    for t in range(NT):
                        yield (l, b, "fwd", t)

        for l in range(depth):
            for s in range(12):
                plan.append((l, S_MOD0 + s))
        for (l, b, kind, t) in tiles_schedule():
            if kind == "bwd":
                plan.extend((l, s) for s in range(5))
            elif kind == "ctx" and l == depth - 1:
                plan.extend((l, s) for s in range(5))
            else:
                plan.extend((l, s) for s in range(34))

        def wnext(l, s):
            n = wstate["next"]
            assert plan[n] == (l, s), (plan[n], (l, s), n)
            wstate["next"] += 1
            while wstate["issued"] < min(len(plan), n + NSLOT):
                j = wstate["issued"]
                pl, ps_ = plan[j]
                slot = wring[j % NSLOT]
                assert wbuf_d[pl][ps_].w is not None and (pl == 0 or ps_ >= 34 or not deferred[pl]), (pl, ps_)
                k.dma(SP, f"w{j % NSLOT}", slot.t[:], wb_d[pl, ps_], ins=[wbuf_d[pl][ps_]], outs=slot.all)
                wstate["issued"] += 1
            return wring[n % NSLOT]

        early = [(0, s) for s in range(NSLAB)]
        for l in range(1, depth):
            early += [(l, S_MISC)] + [(l, S_MOD0 + i) for i in range(12)]
        for n, (l, s) in enumerate(early):
            k.dma(POOL, "castm", wb_d[l, s], ws_d[l, s], ins=[], outs=[wbuf_d[l][s]])
            if n % 4 == 3:
                co = k.dma_owner("castm")
                k.wait(POOL, (co, co.cnt - 32))
        fin = (k.dma_owner("castm"), k.dma_owner("castm").cnt)
        for (l, s) in early:
            wbuf_d[l][s].w = fin
        deferred = {l: [(l, s) for s in range(34)] for l in range(1, depth)}

        def issue_deferred_casts(cur_layer, n):
            l = cur_layer + 1
            if l not in deferred or not deferred[l]:
                return
            for _ in range(n):
                if not deferred[l]:
                    break
                (ll, s) = deferred[l].pop(0)
                k.dma(POOL, f"cast{ll}", wb_d[ll, s], ws_d[ll, s], ins=[], outs=[wbuf_d[ll][s]])
            if not deferred[l]:
                fin_l = (k.dma_owner(f"cast{l}"), k.dma_owner(f"cast{l}").cnt)
                for s in range(34):
                    wbuf_d[l][s].w = fin_l
        k.dma(POOL, "const", cm32.t[:], cm_d[:, 0, :], outs=cm32.all)
        k.dma(POOL, "const", cmb.t[:], cm_d, outs=cmb.all)
        k.dma(POOL, "const", vec.t[:], vec_d, outs=vec.all)
        k.dma(POOL, "const", cT.t[:], cT_d, outs=cT.all)
        for l in range(depth):
            k.dma(POOL, "const", lwt.t[:, l, :], wb_d[l, S_MISC][:, 0:1024], ins=[wbuf_d[l][S_MISC]], outs=lwt.all)
            k.dma(POOL, "const", wgbt.t[:, l, :], wb_d[l, S_MISC][0:33, 2048:2560], ins=[wbuf_d[l][S_MISC]], outs=wgbt.all)
        fin = (k.dma_owner("const"), k.dma_owner("const").cnt)
        for tl in (cm32, cmb, vec, cT, lwt, wgbt):
            tl.b[0].w = fin

        for dr in range(2):
            for h in range(HEADS):
                cp(DVE, maskr.t[:, dr, h, :], cmb.t[:, 5 + dr, :], cmb.all, maskr.all)
        ident32 = cm32.t[:]
        identb = cmb.t[:, 0, :]
        Umat = [cmb.t[:, 1, :], cmb.t[:, 2, :]]
        W2mat = [cmb.t[:, 3, :], cmb.t[:, 4, :]]
        ones_mean = cmb.t[:, 7, :]
        ones_rms = cmb.t[:, 8, :]
        ones_row = cmb.t[0:1, 9, :]
        CONSTS = cmb.all

        k.op(DVE, lambda: nc.vector.memset(zT.t[:], 0.0), outs=zT.all)
        k.op(DVE, lambda: nc.vector.memset(zT.t[32:33], 1.0), outs=zT.all)
        act(csil.t[:], cT.t[:], AF.Silu, cT.all, csil.all)
        for l in range(depth if stop != "casts" else 0):
            p = getps()
            for i in range(12):
                slot = wnext(l, S_MOD0 + i)
                wv = slot.t[:].rearrange("p (k c) -> p k c", k=KC)
                for m in range(4):
                    ch = i * 4 + m
                    for kc in range(KC):
                        mm(p.t[:, ch * 4:ch * 4 + 4], wv[:, kc, m * 128:(m + 1) * 128], csil.t[:, kc, :],
                           kc == 0, kc == KC - 1, ins=slot.all + csil.all, outs=p.all)
            pv = p.t[:, 0:192].rearrange("p (c j) -> p c j", j=4)
            for j in range(3):
                tt(DVE, mod.t[:, l, j, :], pv[:, :, j], vec.t[:, l, V_BMOD:V_BMOD + 48], ALU.add,
                   ins=p.all + vec.all, outs=mod.all)
            for j in range(3):
                ts(DVE, drv.t[:, l, j, 0:8], mod.t[:, l, j, 8:16], 1.0, 1.0 / ALPHA, ALU.add, ALU.mult,
                   mod.all, drv.all, hazard=True)
                ts(DVE, drv.t[:, l, j, 8:16], mod.t[:, l, j, 32:40], 1.0, 1.0 / ALPHA, ALU.add, ALU.mult,
                   mod.all, drv.all, hazard=True)
            last = (l == depth - 1)
            ts(DVE, drv2.t[:, l, 0:16], vec.t[:, l, V_LN1W:V_LN1W + 16], ALPHA, None, ALU.mult, None, vec.all, drv2.all)
            ts(DVE, drv2.t[:, l, 16:32], vec.t[:, l, V_LN2W:V_LN2W + 16], 1.0 if (last and (depth == DEPTH or force_final)) else ALPHA, None,
               ALU.mult, None, vec.all, drv2.all)
        MODS = mod.all + drv.all + drv2.all + vec.all

        def layer_norm(xt, lw, lb, defer=False):
            for kc in range(KC):
                cp(DVE, ybf.t[:, kc, :], xt.t[:, kc, :], [xt.b[kc]], [ybf.b[kc]])
                act(ysq.t[:, kc, :], xt.t[:, kc, :], AF.Square, [xt.b[kc]], [ysq.b[kc]])
            if defer:
                return lambda: layer_norm_b(xt, lw, lb)
            layer_norm_b(xt, lw, lb)

        def layer_norm_b(xt, lw, lb):
            pm, pq = getps(), getps()
            for kc in range(KC):
                mm(pm.t[:, 0:T], ones_mean, ybf.t[:, kc, :], kc == 0, kc == KC - 1, CONSTS + [ybf.b[kc]], pm.all)
            for kc in range(KC):
                mm(pq.t[:, 0:T], ones_mean, ysq.t[:, kc, :], kc == 0, kc == KC - 1, CONSTS + [ysq.b[kc]], pq.all)
            cp(ACT, lnm.t[:], pm.t[:, 0:T], pm.all, lnm.all)
            act(lnm2.t[:], pm.t[:, 0:T], AF.Square, pm.all, lnm2.all)
            tt(DVE, lnv.t[:], pq.t[:, 0:T], lnm2.t[:], ALU.subtract, pq.all + lnm2.all, lnv.all)
            ts(DVE, lnv.t[:], lnv.t[:], EPS, None, ALU.add, None, lnv.all, lnv.all)
            act(lnr.t[:], lnv.t[:], AF.Ln, lnv.all, lnr.all)
            act(lnr.t[:], lnr.t[:], AF.Exp, lnr.all, lnr.all, scale=-0.5)
            for kc in range(KC):
                tt(DVE, lnt.t[:, kc, :], xt.t[:, kc, :], lnm.t[:], ALU.subtract, [xt.b[kc]] + lnm.all, [lnt.b[kc]])
                sc = lw if isinstance(lw, float) else lw(kc)
                xin_ = [lnt.b[kc]] + lnr.all + ([] if isinstance(lw, float) else MODS)
                if lb is None:
                    stt(DVE, xt.t[:, kc, :], lnt.t[:, kc, :], sc, lnr.t[:], ALU.mult, ALU.mult, xin_, [xt.b[kc]])
                else:
                    stt(DVE, lnt.t[:, kc, :], lnt.t[:, kc, :], sc, lnr.t[:], ALU.mult, ALU.mult, xin_, [lnt.b[kc]])
                    act(xt.t[:, kc, :], lnt.t[:, kc, :], AF.Identity, [lnt.b[kc]] + MODS, [xt.b[kc]], bias=lb(kc))

        def modulate(xt, l, j, which, dst=None):
            ud = u.tiles[u.i if dst is None else dst]
            for kc in range(KC):
                su = drv.t[:, l, j, which * 8 + kc:which * 8 + kc + 1]
                sh = mod.t[:, l, j, which * 24 + kc:which * 24 + kc + 1]
                if kc % 2 == 0:
                    act(ud.t[:, kc, :], xt.t[:, kc, :], AF.Identity, [xt.b[kc]] + MODS, [ud.b[kc]], scale=su, bias=sh)
                else:
                    ts(DVE, ud.t[:, kc, :], xt.t[:, kc, :], su, sh, ALU.mult, ALU.add, [xt.b[kc]] + MODS, [ud.b[kc]])

        def proj_fm(l, s, nout, evac):
            slot = wnext(l, s)
            wv = slot.t[:].rearrange("p (k c) -> p k c", k=KC)
            per = 512 // T
            m = 0
            while m < nout:
                p = getps()
                n_here = min(per, nout - m)
                for q in range(n_here):
                    for kc in range(KC):
                        mm(p.t[:, q * T:(q + 1) * T], wv[:, kc, (m + q) * 128:(m + q + 1) * 128], u.t[:, kc, :],
                           kc == 0, kc == KC - 1, slot.all + [u.b[kc]], p.all)
                evac(m, n_here, p)
                m += n_here

        def proj_tm(l, s, dst, col0):
            slot = wnext(l, s)
            wv = slot.t[:].rearrange("p (k c) -> p k c", k=KC)
            for c in range(NCH):
                p = getps()
                for kc in range(KC):
                    mm(p.t[:, :], u.t[:, kc, c * 128:(c + 1) * 128], wv[:, kc, :], kc == 0, kc == KC - 1,
                       slot.all + [u.b[kc]], p.all)
                cp(ACT if c % 2 == 0 else DVE, dst.t[:, c, col0:col0 + 512], p.t[:, :], p.all, [dst.b[c]])

        def gla_stage1(l, c, dr, par):
            cs = slice(c * 128, (c + 1) * 128)
            wg = lwt.t[0:33, l, 512:1024] if dr == 0 else wgbt.t[0:33, l, :]
            p = getps()
            mm(p.t[:, :], zT.t[:, dr, cs], wg, True, True, zT.all + lwt.all + wgbt.all, p.all)
            act(e1[par].t[:], p.t[:, :], AF.Exp, p.all, e1[par].all, scale=-1.0)
            act(sp_[par].t[:], e1[par].t[:], AF.Ln, e1[par].all, sp_[par].all, bias=1.0)
            ck('s1a')
            p2 = getps()
            for h in range(HEADS):
                mm(p2.t[:, h * 128:(h + 1) * 128], sp_[par].t[:, h * 128:(h + 1) * 128], Umat[dr], True, True,
                   sp_[par].all + CONSTS, p2.all, sig=(h == HEADS - 1))
            p2v = p2.t[:, :].rearrange("p (h i) -> p h i", h=HEADS)
            act(Ep[par].t[:], p2v, AF.Exp, p2.all, Ep[par].all)
            act(Em[par].t[:], p2v, AF.Exp, p2.all, Em[par].all, scale=-1.0)
            ck('s1b')
            tt(DVE, qe[par].t[:], qT.t[:, :, cs], Ep[par].t[:], ALU.mult, qT.all + Ep[par].all, qe[par].all)
            tt(DVE, ke[par].t[:], kT.t[:, :, cs], Em[par].t[:], ALU.mult, kT.all + Em[par].all, ke[par].all)
            ck('s1c')
            p3 = getps()
            mm(p3.t[:, :], W2mat[dr], sp_[par].t[:], True, True, sp_[par].all + CONSTS, p3.all)
            act(er[par].t[:], p3.t[:, :], AF.Exp, p3.all, er[par].all)
            ck('s1d')
            pb = pst_t[:, par * 512:(par + 1) * 512]
            for h in range(HEADS):
                k.op(PE, lambda h=h: nc.tensor.transpose(out=pst_t[:, par * 512 + h * 128:par * 512 + (h + 1) * 128],
                                                          in_=kT.t[:, h, cs], identity=identb),
                     ins=kT.all + CONSTS, outs=[pst[par]], sig=(h == HEADS - 1))
            tt(DVE, kend[par].t[:], pb, er[par].t[:], ALU.mult, [pst[par]] + er[par].all, kend[par].all)
            ck('s1e')
            p4 = getps()
            for h in range(HEADS):
                mm(p4.t[:, h * 128:(h + 1) * 128], ke[par].t[:, h, :], qe[par].t[:, h, :], True, True,
                   ke[par].all + qe[par].all, p4.all, sig=(h == HEADS - 1))
            tt(DVE, attm[par].t[:], p4.t[:, :].rearrange("p (h i) -> p h i", h=HEADS), maskr.t[:, dr], ALU.mult,
               p4.all + maskr.all, attm[par].all)
            ck('s1f')

        def gla_stage2(c, dr, par, omode):
            cs = slice(c * 128, (c + 1) * 128)
            S, Sb_ = Sst[dr], Sbf[dr]
            if omode is not None:
                for half in range(2):
                    p = getps()
                    for q in range(4):
                        hv = half * 4 + q
                        h, vc = hv // 2, hv % 2
                        mm(p.t[:, q * 128:(q + 1) * 128], vtm.t[:, c, h * 256 + vc * 128:h * 256 + (vc + 1) * 128],
                           attm[par].t[:, h, :], True, False, [vtm.b[c]] + attm[par].all, p.all, sig=False)
                        mm(p.t[:, q * 128:(q + 1) * 128], Sb_.t[:, h, vc * 128:(vc + 1) * 128], qe[par].t[:, h, :],
                           False, True, Sb_.all + qe[par].all, p.all, sig=(q == 3))
                    ck('s2m')
                    pv = p.t[:, :].rearrange("p (q i) -> p q i", q=4)
                    ov = o.t[:, half * 4:(half + 1) * 4, cs]
                    if omode == "copy":
                        cp(ACT, ov, pv, p.all, o.all)
                    else:
                        tt(DVE, ov, pv, ov, ALU.add, p.all + o.all, o.all)
                ck('s2a')
            last = 127 if dr == 0 else 0
            for half in range(2):
                p = getps()
                for q in range(2):
                    h = half * 2 + q
                    mm(p.t[:, q * 256:(q + 1) * 256], kend[par].t[:, h * 128:(h + 1) * 128], vtm.t[:, c, h * 256:(h + 1) * 256],
                       True, True, kend[par].all + [vtm.b[c]], p.all, sig=(q == 1))
                ck('s2d')
                for q in range(2):
                    h = half * 2 + q
                    stt(DVE, S.t[:, h, :], S.t[:, h, :], Ep[par].t[:, h, last:last + 1], p.t[:, q * 256:(q + 1) * 256],
                        ALU.mult, ALU.add, S.all + Ep[par].all + p.all, S.all)
                ck('s2s')
            cp(ACT, Sb_.t[:], S.t[:], S.all, Sb_.all)
            ck('s2c')

        def gla_scan(l, dr, omode, fill=()):
            fill = list(fill)

            def filler():
                if fill:
                    fill.pop(0)()
            order = list(range(NCH)) if dr == 0 else list(range(NCH - 1, -1, -1))
            gla_stage1(l, order[0], dr, 0)
            filler()
            for n, c in enumerate(order):
                if n + 1 < len(order):
                    gla_stage1(l, order[n + 1], dr, (n + 1) % 2)
                    filler()
                gla_stage2(c, dr, n % 2, omode)
                filler()
            while fill:
                filler()

        def proj_gla_inputs(l):
            def ev_z(m, n, p):
                cp(ACT, zT.t[0:16, :, :], p.t[0:16, 0:2 * T].rearrange("p (a t) -> p a t", a=2), p.all, zT.all)
            slot = wnext(l, S_Z)
            wv = slot.t[:].rearrange("p (k c) -> p k c", k=KC)
            p = getps()
            for zz in range(2):
                for kc in range(KC):
                    mm(p.t[0:16, zz * T:(zz + 1) * T], wv[:, kc, zz * 32:zz * 32 + 16], u.t[:, kc, :], kc == 0, kc == KC - 1,
                       slot.all + [u.b[kc]], p.all)
            ev_z(0, 1, p)

            def ev_q(m, n, p):
                k.op(ACT, lambda: nc.scalar.mul(out=qT.t[:, m:m + n, :], in_=p.t[:, 0:n * T].rearrange("p (a t) -> p a t", a=n),
                                                mul=QSCALE), ins=p.all, outs=qT.all)
            proj_fm(l, S_Q, 4, ev_q)

            def ev_k(m, n, p):
                cp(DVE, kT.t[:, m:m + n, :], p.t[:, 0:n * T].rearrange("p (a t) -> p a t", a=n), p.all, kT.all)
            proj_fm(l, S_K, 4, ev_k)

        def proj_v(l):
            proj_tm(l, S_V0, vtm, 0)
            proj_tm(l, S_V1, vtm, 512)

        def load_xa(b, t, par):
            k.dma(POOL, f"xa{par}", xa[par].t[:], xs_d[b, t], ins=[xsb[b][t]], outs=xa[par].all)

        xsb = [[Buf(f"xs{b}_{t}") for t in range(NT + 1)] for b in range(NBC)]
        obb = [[Buf(f"ob{b}_{t}") for t in range(NT)] for b in range(NBC)]
        posb = Buf("posd")
        xind = Buf("xind")
        post = otmp

        par = 0
        for b in range(NBC if stop not in ("prologue", "casts") else 0):
            for t in range(NT + 1):
                X = xa[par]
                if t < NT:
                    src = x_d[b, t * T:(t + 1) * T, :].rearrange("(c p) d -> p c d", p=128)
                    k.dma(POOL, "pos", post.t[:], pos_d[t], ins=[posb], outs=post.all)
                else:
                    src = ctx_d[b].rearrange("(c p) d -> p c d", p=128)
                k.dma(POOL, "xin", xin.t[:], src, ins=[xind], outs=xin.all)
                for kc in range(0, KC, 512 // T):
                    p = getps()
                    nk = 512 // T
                    for q in range(nk):
                        for c in range(NCH):
                            k.op(PE, lambda q=q, c=c, kc=kc, p=p: nc.tensor.transpose(
                                out=p.t[:, q * T + c * 128:q * T + (c + 1) * 128],
                                in_=xin.t[:, c, (kc + q) * 128:(kc + q + 1) * 128], identity=ident32),
                                ins=xin.all + cm32.all, outs=p.all, sig=(q == nk - 1 and c == NCH - 1))
                    pv = p.t[:, 0:nk * T].rearrange("p (a t) -> p a t", a=nk)
                    if t < NT:
                        tt(DVE, X.t[:, kc:kc + nk, :], pv, post.t[:, kc:kc + nk, :], ALU.add, p.all + post.all,
                           X.b[kc:kc + nk])
                    else:
                        cp(DVE, X.t[:, kc:kc + nk, :], pv, p.all, X.b[kc:kc + nk])
                layer_norm(X, ALPHA, None)
                k.dma(POOL, f"xst{par}", xs_d[b, t], X.t[:], ins=X.all, outs=[xsb[b][t]])
                par ^= 1

        try:
            sched = [] if stop in ("none", "casts", "prologue", "phase0") else list(tiles_schedule())
            if ntiles is not None:
                sched = sched[:ntiles]
            pending = [None]

            def run_pending():
                if pending[0] is not None:
                    f = pending[0]
                    pending[0] = None
                    f()

            def ev_r(base, dst, func):
                def f(m, n, p):
                    act(dst.t[:, base + m:base + m + n, :], p.t[:, 0:n * T].rearrange("p (a t) -> p a t", a=n), func,
                        p.all, dst.b[base + m:base + m + n])
                return f

            if sched:
                l0, b0, kind0, t0 = sched[0]
                load_xa(b0, t0, par)
                modulate(xa[par], l0, 2 if kind0 == "ctx" else b0, 0, dst=par)
            for ti, (l, b, kind, t) in enumerate(sched):
                j = 2 if kind == "ctx" else b
                last_layer = (l == depth - 1)
                X = xa[par]
                u.i = par
                nxt = sched[ti + 1] if ti + 1 < len(sched) else None

                def pre_mod(nxt=nxt, par=par):
                    if nxt is not None:
                        modulate(xa[par ^ 1], nxt[0], 2 if nxt[2] == "ctx" else nxt[1], 0, dst=par ^ 1)
                ck('mod')
                if kind == "ctx":
                    k.op(DVE, lambda: nc.vector.memset(Sst[0].t[:], 0.0), outs=Sst[0].all)
                    k.op(DVE, lambda: nc.vector.memset(Sst[1].t[:], 0.0), outs=Sst[1].all)
                    k.op(DVE, lambda: nc.vector.memset(Sbf[0].t[:], 0.0), outs=Sbf[0].all)
                    k.op(DVE, lambda: nc.vector.memset(Sbf[1].t[:], 0.0), outs=Sbf[1].all)
                proj_gla_inputs(l)
                run_pending()
                if nxt is not None:
                    load_xa(nxt[1], nxt[3], par ^ 1)
                issue_deferred_casts(l, 2)
                ck('proj')
                if kind == "bwd":
                    gla_scan(l, 1, "copy", fill=[lambda: proj_v(l), pre_mod])
                    k.dma(POOL, "ost", ob_d[b, t], o.t[:], ins=o.all, outs=[obb[b][t]])
                    par ^= 1
                    continue
                if kind == "ctx":
                    if last_layer:
                        gla_scan(l, 0, None, fill=[lambda: proj_v(l)])
                        gla_scan(l, 1, None, fill=[pre_mod])
                        par ^= 1
                        continue
                    gla_scan(l, 0, "copy", fill=[lambda: proj_v(l)])
                    ck('gla1')
                    gla_scan(l, 1, "add")
                    ck('gla2')
                    proj_tm(l, S_POOL, ptm, 0)
                    proj_fm(l, S_R0, 4, ev_r(0, rs, AF.Silu))
                    proj_fm(l, S_R0 + 1, 4, ev_r(4, rs, AF.Silu))
                    proj_fm(l, S_GP0, 4, ev_r(0, gp, AF.Sigmoid))
                    proj_fm(l, S_GP0 + 1, 4, ev_r(4, gp, AF.Sigmoid))
                    proj_fm(l, S_GG0, 4, ev_r(0, gg, AF.Sigmoid))
                    proj_fm(l, S_GG0 + 1, 4, ev_r(4, gg, AF.Sigmoid))
                else:
                    k.dma(POOL, "ob", o.t[:], ob_d[b, t], ins=[obb[b][t]], outs=o.all)

                    def f1():
                        proj_tm(l, S_POOL, ptm, 0)
                        proj_fm(l, S_R0, 4, ev_r(0, rs, AF.Silu))
                        proj_fm(l, S_R0 + 1, 4, ev_r(4, rs, AF.Silu))

                    def f2():
                        proj_fm(l, S_GP0, 4, ev_r(0, gp, AF.Sigmoid))
                        proj_fm(l, S_GP0 + 1, 4, ev_r(4, gp, AF.Sigmoid))

                    def f3():
                        proj_fm(l, S_GG0, 4, ev_r(0, gg, AF.Sigmoid))
                        proj_fm(l, S_GG0 + 1, 4, ev_r(4, gg, AF.Sigmoid))
                    gla_scan(l, 0, "add", fill=[lambda: proj_v(l), f1, f2, f3])

                ck('inproj')
                act(osq.t[:], o.t[:], AF.Square, o.all, osq.all)
                for h in range(HEADS):
                    p = getps()
                    mm(p.t[:, 0:T], ones_rms, osq.t[:, 2 * h, :], True, False, CONSTS + osq.all, p.all, sig=False)
                    mm(p.t[:, 0:T], ones_rms, osq.t[:, 2 * h + 1, :], False, True, CONSTS + osq.all, p.all)
                    ts(DVE, rstd.t[:, h, :], p.t[:, 0:T], EPS, None, ALU.add, None, p.all, rstd.all)
                act(rstd.t[:], rstd.t[:], AF.Ln, rstd.all, rstd.all)
                act(rstd.t[:], rstd.t[:], AF.Exp, rstd.all, rstd.all, scale=-0.5)
                for hv in range(KC):
                    tt(DVE, otmp.t[:, hv, :], o.t[:, hv, :], rstd.t[:, hv // 2, :], ALU.mult, o.all + rstd.all, [otmp.b[hv]])
                    stt(DVE, of.t[:, hv, :], otmp.t[:, hv, :], vec.t[:, l, V_GNW + hv:V_GNW + hv + 1],
                        rs.t[:, hv, :], ALU.mult, ALU.mult, [otmp.b[hv], rs.b[hv]] + vec.all, [of.b[hv]])

                ck('rms')
                for g in range(4):
                    p = getps()
                    for tc in range(NCH):
                        if kind == "ctx":
                            for jc in range(NCH):
                                mm(p.t[:, tc * 128:(tc + 1) * 128], ptm.t[:, jc, g * 128:(g + 1) * 128],
                                   cmb.t[:, 14 + g * 4 + jc * 2 + tc, :], jc == 0, jc == NCH - 1, [ptm.b[jc]] + CONSTS, p.all,
                                   sig=(jc == NCH - 1 and tc == NCH - 1))
                        else:
                            mm(p.t[:, tc * 128:(tc + 1) * 128], ptm.t[:, tc, g * 128:(g + 1) * 128], cmb.t[:, 10 + g, :],
                               True, True, [ptm.b[tc]] + CONSTS, p.all, sig=(tc == NCH - 1))
                    cp(ACT, pooledT.t[:, g, :], p.t[:, 0:T], p.all, [pooledT.b[g]])
                for g in range(4):
                    p = getps()
                    mm(p.t[:, 0:T], lwt.t[:, l, g * 128:(g + 1) * 128], pooledT.t[:, g, :], True, True,
                       lwt.all + [pooledT.b[g]], p.all)
                    ts(DVE, pooled2.t[:, g, :], p.t[:, 0:T], vec.t[:, l, V_PSC + g:V_PSC + g + 1], None, ALU.mult, None,
                       p.all + vec.all, [pooled2.b[g]])
                slot = wnext(l, S_BRPOOL)
                wv = slot.t[:].rearrange("p (g c) -> p g c", g=4)
                for m in range(KC):
                    p = getps()
                    for g in range(4):
                        mm(p.t[:, 0:T], wv[:, g, m * 128:(m + 1) * 128], pooled2.t[:, g, :], g == 0, g == 3,
                           slot.all + [pooled2.b[g]], p.all)
                    tt(DVE, t1.t[:, m, :], p.t[:, 0:T], gp.t[:, m, :], ALU.mult, p.all + [gp.b[m]], [t1.b[m]])

                ck('pool')
                def gen_fm(l, s0, src, evac):
                    for sidx in range(2):
                        slot = wnext(l, s0 + sidx)
                        wv = slot.t[:].rearrange("p (k c) -> p k c", k=KC)
                        for mq in range(4):
                            m = sidx * 4 + mq
                            p = getps()
                            for kc in range(KC):
                                mm(p.t[:, 0:T], wv[:, kc, mq * 128:(mq + 1) * 128], src.t[:, kc, :], kc == 0, kc == KC - 1,
                                   slot.all + [src.b[kc]], p.all)
                            evac(m, p)

                def ev_gla(m, p):
                    tt(DVE, t2.t[:, m, :], p.t[:, 0:T], gg.t[:, m, :], ALU.mult, p.all + [gg.b[m]], [t2.b[m]])
                    tt(POOL, merged.t[:, m, :], t1.t[:, m, :], t2.t[:, m, :], ALU.add, [t1.b[m], t2.b[m]], [merged.b[m]])
                gen_fm(l, S_BRGLA0, of, ev_gla)

                def ev_mix(m, p):
                    stt(DVE, X.t[:, m, :], p.t[:, 0:T], mod.t[:, l, j, 16 + m:17 + m], X.t[:, m, :], ALU.mult, ALU.add,
                        p.all + [X.b[m]] + MODS, [X.b[m]])
                gen_fm(l, S_OUT0, merged, ev_mix)
                ck('mix')
                layer_norm(X, lambda kc: drv2.t[:, l, kc:kc + 1], lambda kc: drv2.t[:, l, 8 + kc:9 + kc])
                ck('ln1')
                modulate(X, l, j, 1)

                rows, rl = (T // 64, 64) if kind == "fwd" else (1, T)
                for s in range(11):
                    slot = wnext(l, S_UP0 + s)
                    wv = slot.t[:].rearrange("p (k c) -> p k c", k=KC)
                    bp = s % 2
                    pa, pg = getps(), getps()
                    for q in range(4):
                        p = pa if q < 2 else pg
                        for kc in range(KC):
                            mm(p.t[:, (q % 2) * T:(q % 2 + 1) * T], wv[:, kc, q * 128:(q + 1) * 128], u.t[:, kc, :], kc == 0,
                               kc == KC - 1, slot.all + [u.b[kc]], p.all)
                    A, C_, G = asb[bp], acc[bp], gel[bp]
                    cp(ACT, A.t[:], pa.t[:, 0:2 * T].rearrange("p (a t) -> p a t", a=2), pa.all, A.all)
                    for q in range(2):
                        fc = 2 * s + q
                        act(C_.t[:, q, :], pa.t[:, q * T:(q + 1) * T], AF.Identity, pa.all + vec.all, C_.all,
                            scale=vec.t[:, l, V_CW + 22 + fc:V_CW + 23 + fc], bias=vec.t[:, l, V_CB + fc:V_CB + fc + 1])
                        av = A.t[:, q, :].rearrange("p (r w) -> p r w", w=rl)
                        cv = C_.t[:, q, :].rearrange("p (r w) -> p r w", w=rl)
                        stt(DVE, cv[:, :, 1:rl], av[:, :, 0:rl - 1], vec.t[:, l, V_CW + fc:V_CW + fc + 1], cv[:, :, 1:rl],
                            ALU.mult, ALU.add, A.all + C_.all + vec.all, C_.all)
                        stt(DVE, cv[:, :, 0:rl - 1], av[:, :, 1:rl], vec.t[:, l, V_CW + 44 + fc:V_CW + 45 + fc],
                            cv[:, :, 0:rl - 1], ALU.mult, ALU.add, A.all + C_.all + vec.all, C_.all)
                    act(G.t[:], C_.t[:], AF.Gelu, C_.all, G.all)
                    hT = (rs, gp, gg)[(2 * s) // 8]
                    hi = (2 * s) % 8
                    tt(DVE, hT.t[:, hi:hi + 2, :], pg.t[:, 0:2 * T].rearrange("p (a t) -> p a t", a=2), G.t[:], ALU.mult,
                       pg.all + G.all, hT.b[hi:hi + 2])
                ck('up')
                pre_mod()
                per = 512 // T
                pacc = [getps() for _ in range(KC // per)]
                if per > 1:
                    for p in pacc:
                        mm(p.t[:, :], cmb.t[:, 30, :], cmb.t[:, 30:34, :], True, True, CONSTS, p.all)
                for s in range(6):
                    slot = wnext(l, S_DOWN0 + s)
                    wv = slot.t[:].rearrange("p (k c) -> p k c", k=4)
                    nk = 4 if s < 5 else 2
                    for m in range(KC):
                        p = pacc[m // per]
                        for q in range(nk):
                            fc = 4 * s + q
                            hT = (rs, gp, gg)[fc // 8]
                            mm(p.t[:, (m % per) * T:(m % per + 1) * T], wv[:, q, m * 128:(m + 1) * 128], hT.t[:, fc % 8, :],
                               (fc == 0 and per == 1), fc == FC - 1, slot.all + [hT.b[fc % 8]], p.all, sig=(q == nk - 1))
                for m in range(KC):
                    p = pacc[m // per]
                    stt(DVE, X.t[:, m, :], p.t[:, (m % per) * T:(m % per + 1) * T], mod.t[:, l, j, 40 + m:41 + m], X.t[:, m, :],
                        ALU.mult, ALU.add, p.all + [X.b[m]] + MODS, [X.b[m]])
                lnb = layer_norm(X, lambda kc, l=l: drv2.t[:, l, 16 + kc:17 + kc],
                                 (lambda kc, l=l: drv2.t[:, l, 24 + kc:25 + kc]), defer=True)
                is_final = last_layer and (depth == DEPTH or force_final) and kind == "fwd"

                def tail(lnb=lnb, X=X, b=b, t=t, par=par, is_final=is_final):
                    lnb()
                    if is_final:
                        for c in range(NCH):
                            for kc0 in range(0, KC, 4):
                                p = getps()
                                for q in range(4):
                                    kc = kc0 + q
                                    k.op(PE, lambda q=q, kc=kc, c=c, p=p: nc.tensor.transpose(
                                        out=p.t[:, q * 128:(q + 1) * 128], in_=X.t[:, kc, c * 128:(c + 1) * 128], identity=ident32),
                                        ins=[X.b[kc]] + cm32.all, outs=p.all, sig=(q == 3))
                                cp(ACT if kc0 else DVE, xout.t[:, c, kc0 * 128:(kc0 + 4) * 128], p.t[:, :], p.all, xout.all)
                        dst = out_d[b, t * T:(t + 1) * T, :].rearrange("(c p) d -> p c d", p=128)
                        k.dma(POOL, "outst", dst, xout.t[:], ins=xout.all, outs=[Buf("outd")])
                    else:
                        k.dma(POOL, f"xst{par}", xs_d[b, t], X.t[:], ins=X.all, outs=[xsb[b][t]])
                pending[0] = tail
                par ^= 1
            run_pending()
        except StopBuild:
            pass
        assert stop or ntiles is not None or wstate["next"] == len(plan), (wstate, len(plan))
        for name, ds in k.dsem.items():
            if ds.cnt:
                k.wait(SP, (ds, ds.cnt))
        for E in (PE, ACT, DVE, POOL):
            if E.cnt:
                k.wait(SP, (E, E.cnt))
    return nc


def _pos_embed():
    rows, cols, dim = L // 64, 64, D
    quarter = dim // 4
    omega = (1.0 / (10000.0 ** (np.arange(quarter, dtype=np.float32) / np.float32(quarter)))).astype(np.float32)
    r = np.arange(rows, dtype=np.float32)[:, None] * omega
    cl = np.arange(cols, dtype=np.float32)[:, None] * omega
    er = np.concatenate([np.sin(r), np.cos(r)], -1).astype(np.float32)
    ec = np.concatenate([np.sin(cl), np.cos(cl)], -1).astype(np.float32)
    emb = np.concatenate([np.broadcast_to(er[:, None, :], (rows, cols, dim // 2)),
                          np.broadcast_to(ec[None, :, :], (rows, cols, dim // 2))], -1)
    return emb.reshape(rows * cols, dim).astype(np.float32)


def _pool_matrix(n, w):
    P = np.zeros((n, n), np.float32)
    for t in range(n):
        lo, hi = max(t - w // 2, 0), min(t + w // 2, n)
        P[t, lo:hi] = 1.0 / (hi - lo)
        P[t, t] -= 1.0
    return P


def _const_mats():
    cm = np.zeros((34, 128, 128), np.float32)
    j = np.arange(128)[:, None]
    i = np.arange(128)[None, :]
    cm[0] = (j == i)
    cm[1] = (j <= i) * (-1.0 / 16.0)
    cm[2] = (j >= i) * (-1.0 / 16.0)
    cm[3] = (j > i) * (-1.0 / 16.0)
    cm[4] = (j < i) * (-1.0 / 16.0)
    cm[5] = (j <= i)
    cm[6] = (j >= i)
    cm[7] = 1.0 / 1024.0
    cm[8] = 1.0 / 256.0
    cm[9] = 1.0
    for g, w in enumerate((2, 4, 8, 16)):
        P64 = _pool_matrix(64, w)
        Pg = np.zeros((128, 128), np.float32)
        Pg[0:64, 0:64] = P64
        Pg[64:128, 64:128] = P64
        cm[10 + g] = Pg.T
        Pc = _pool_matrix(256, w).T
        for jc in range(2):
            for tc in range(2):
                cm[14 + g * 4 + jc * 2 + tc] = Pc[jc * 128:(jc + 1) * 128, tc * 128:(tc + 1) * 128]
    return np.ascontiguousarray(cm.transpose(1, 0, 2))


def _k1024_slab(W):
    return np.ascontiguousarray(W.reshape(KC, 128, 512).transpose(1, 0, 2)).reshape(128, SLABW)


def _layer_slabs(w_in, w_gate_f, b_gate_f, w_gate_b, b_gate_b, w_pool, w_br_pool, w_br_gla, w_out, w_up, w_down, w_mod):
    S = np.zeros((NSLAB, 128, SLABW), np.float32)
    o_pool, o_q, o_k, o_v, o_r, o_zf, o_zb, o_gp, o_gg = 0, 512, 1024, 1536, 2560, 3584, 3600, 3616, 4640
    zc = np.zeros((1024, 512), np.float32)
    zc[:, 0:16] = w_in[:, o_zf:o_zf + 16]
    zc[:, 32:48] = w_in[:, o_zb:o_zb + 16]
    S[S_Z] = _k1024_slab(zc)
    S[S_Q] = _k1024_slab(w_in[:, o_q:o_q + 512])
    S[S_K] = _k1024_slab(w_in[:, o_k:o_k + 512])
    S[S_V0] = _k1024_slab(w_in[:, o_v:o_v + 512])
    S[S_V1] = _k1024_slab(w_in[:, o_v + 512:o_v + 1024])
    S[S_POOL] = _k1024_slab(w_in[:, o_pool:o_pool + 512])
    for i in range(2):
        S[S_R0 + i] = _k1024_slab(w_in[:, o_r + 512 * i:o_r + 512 * (i + 1)])
        S[S_GP0 + i] = _k1024_slab(w_in[:, o_gp + 512 * i:o_gp + 512 * (i + 1)])
        S[S_GG0 + i] = _k1024_slab(w_in[:, o_gg + 512 * i:o_gg + 512 * (i + 1)])
        S[S_BRGLA0 + i] = _k1024_slab(w_br_gla[:, 512 * i:512 * (i + 1)])
        S[S_OUT0 + i] = _k1024_slab(w_out[:, 512 * i:512 * (i + 1)])
    S[S_BRPOOL] = np.ascontiguousarray(w_br_pool.reshape(4, 128, 1024).transpose(1, 0, 2)).reshape(128, SLABW)
    F = 128 * FC
    for s in range(11):
        Wc = np.concatenate([w_up[:, s * 256:(s + 1) * 256], w_up[:, F + s * 256:F + (s + 1) * 256]], axis=1)
        S[S_UP0 + s] = _k1024_slab(Wc)
    wd = np.zeros((24 * 128, 1024), np.float32)
    wd[:F] = w_down
    for s in range(6):
        S[S_DOWN0 + s] = np.ascontiguousarray(wd[s * 512:(s + 1) * 512].reshape(4, 128, 1024).transpose(1, 0, 2)).reshape(128, SLABW)
    misc = np.zeros((128, SLABW), np.float32)
    misc[:, 0:512] = w_pool.transpose(1, 0, 2).reshape(128, 512)
    misc[0:16, 512:1024] = w_gate_f
    misc[0:16, 2048:2560] = w_gate_b
    misc[32, 512:1024] = b_gate_f
    misc[32, 2048:2560] = b_gate_b
    S[S_MISC] = misc
    for i in range(12):
        S[S_MOD0 + i] = _k1024_slab(w_mod[:, 512 * i:512 * (i + 1)])
    return S


def _chunkvec(v):
    return np.ascontiguousarray(v.reshape(-1, 128).T)


def _layer_vec(b_mod, ln1_w, ln1_b, ln2_w, ln2_b, gla_norm_w, pool_scale, conv_w, conv_b):
    V = np.zeros((128, NVEC), np.float32)
    V[:, V_BMOD:V_BMOD + 48] = _chunkvec(b_mod)
    V[:, V_LN1W:V_LN1W + 8] = _chunkvec(ln1_w)
    V[:, V_LN1B:V_LN1B + 8] = _chunkvec(ln1_b)
    V[:, V_LN2W:V_LN2W + 8] = _chunkvec(ln2_w)
    V[:, V_LN2B:V_LN2B + 8] = _chunkvec(ln2_b)
    V[:, V_GNW:V_GNW + 8] = _chunkvec(gla_norm_w)
    V[:, V_PSC:V_PSC + 4] = _chunkvec(pool_scale)
    for kk in range(3):
        V[:, V_CW + 22 * kk:V_CW + 22 * (kk + 1)] = _chunkvec(conv_w[kk])
    V[:, V_CB:V_CB + 22] = _chunkvec(conv_b)
    return V


def prepare_inputs(inp, depth=DEPTH, ncore=NCORE):
    f = lambda a: np.asarray(a, dtype=np.float32)
    slabs = np.stack([_layer_slabs(f(inp["w_in"][l]), f(inp["w_gate_f"][l]), f(inp["b_gate_f"][l]), f(inp["w_gate_b"][l]),
                                   f(inp["b_gate_b"][l]), f(inp["w_pool"][l]), f(inp["w_br_pool"][l]), f(inp["w_br_gla"][l]),
                                   f(inp["w_out"][l]), f(inp["w_up"][l]), f(inp["w_down"][l]), f(inp["w_mod"][l]))
                      for l in range(depth)])
    vec = np.stack([_layer_vec(f(inp["b_mod"][l]), f(inp["ln1_w"][l]), f(inp["ln1_b"][l]), f(inp["ln2_w"][l]),
                               f(inp["ln2_b"][l]), f(inp["gla_norm_w"][l]), f(inp["pool_scale"][l]), f(inp["conv_w"][l]),
                               f(inp["conv_b"][l])) for l in range(depth)], axis=1)
    vec = np.ascontiguousarray(vec)
    pos = _pos_embed()
    posT = np.ascontiguousarray(pos.reshape(NT, T, KC, 128).transpose(0, 3, 2, 1))
    cm = _const_mats()
    x, c, ctx, c_ctx = f(inp["x"]), f(inp["c"]), f(inp["ctx"]), f(inp["c_ctx"])
    maps = []
    for i in range(ncore):
        cc = np.stack([c[NBC * i], c[NBC * i + 1], c_ctx, np.zeros_like(c_ctx)], axis=0)
        cT = np.ascontiguousarray(cc.reshape(4, KC, 128).transpose(2, 1, 0))
        maps.append({"x": np.ascontiguousarray(x[NBC * i:NBC * (i + 1)]),
                     "ctx": np.ascontiguousarray(ctx[NBC * i:NBC * (i + 1)]),
                     "cT": cT, "posT": posT, "wslab": slabs, "vec": vec, "cmat": cm})
    return maps


_NC_CACHE = {}


def kernel(**inputs):
    maps = prepare_inputs(inputs)
    if "nc" not in _NC_CACHE:
        _NC_CACHE["nc"] = build_program()
    nc = _NC_CACHE["nc"]
    res = run_bass_kernel_spmd(nc, maps, core_ids=list(range(NCORE)))
    return np.concatenate([r["out"] for r in res.results], axis=0).astype(np.float32)
```
